# Optimizing a Trainium2 kernel written in Bass

```python
import jax, jax.numpy as jnp
from jax import lax
import numpy as np

D_MODEL = 1024
BATCH = 32
SEQ = 256
DEPTH = 4
DEC_BATCH = 8
DEC_SEQ = 2048
PAST_LEN = 512

GRID_W = 64
N_MIXERS = 4
N_REPEAT = DEPTH // N_MIXERS
HEAD_DIM = 64
N_HEADS = D_MODEL // HEAD_DIM
N_KV_HEADS = N_HEADS // 4
Q_BLOCK = 128
NA_WIN_ROWS = 8
NA_WIN_COLS = 16
NA_QCOLS = 16
NA_KCOLS = 32
SWA_WINDOW = 128
MLA_Q_LORA = 256
MLA_KV_LORA = 128
MLA_NOPE = 64
MLA_ROPE = 32
MLA_V = 64
FFN_HIDDEN = ((8 * D_MODEL + 3 * 256 - 1) // (3 * 256)) * 256
ROPE_THETA = 10000.0
EPS = 1e-6
NEG_INF = -1e30

kernel_name = 'hybrid_dit_prefix_context_step'


def rms_norm(x, g):
    xf = x.astype(jnp.float32)
    y = xf * lax.rsqrt(jnp.mean(xf * xf, axis=-1, keepdims=True) + EPS)
    return (y * g.astype(jnp.float32)).astype(x.dtype)


def adaln(cond, w_mod, b_mod):
    m = jax.nn.silu(cond) @ w_mod + b_mod
    return [t[:, None, :] for t in jnp.split(m, 6, axis=-1)]


def modulate(h, shift, scale):
    return h * (1 + scale) + shift


def swiglu(h, w_gate, w_up, w_down):
    return (jax.nn.silu(h @ w_gate) * (h @ w_up)) @ w_down


def grid_positions(n):
    t = jnp.arange(n, dtype=jnp.int32)
    return t // GRID_W, t % GRID_W


def axial_rope(x, rows, cols):
    half = x.shape[-1] // 2
    quarter = half // 2
    inv_freq = ROPE_THETA ** (-jnp.arange(quarter, dtype=jnp.float32) / quarter)

    def rotate(xh, pos):
        ang = pos.astype(jnp.float32)[:, None] * inv_freq[None, :]
        cos, sin = jnp.cos(ang)[:, None, :], jnp.sin(ang)[:, None, :]
        x1 = xh[..., :quarter].astype(jnp.float32)
        x2 = xh[..., quarter:].astype(jnp.float32)
        return jnp.concatenate([x1 * cos - x2 * sin, x2 * cos + x1 * sin], axis=-1)

    out = jnp.concatenate([rotate(x[..., :half], rows), rotate(x[..., half:], cols)], axis=-1)
    return out.astype(x.dtype)


def joint_softmax(parts, sink=None):
    s = jnp.concatenate(parts, axis=-1) if len(parts) > 1 else parts[0]
    m = jnp.max(s, axis=-1, keepdims=True)
    if sink is not None:
        m = jnp.maximum(m, sink)
    p = jnp.exp(s - m)
    den = jnp.sum(p, axis=-1, keepdims=True)
    if sink is not None:
        den = den + jnp.exp(sink - m)
    p = p / den
    out, off = [], 0
    for t in parts:
        n = t.shape[-1]
        out.append(p[..., off:off + n])
        off += n
    return out


def block_attention(q, k, v, sink=None):
    B, Lq, Hq, dk = q.shape
    Hkv, dv = k.shape[2], v.shape[-1]
    G = Hq // Hkv
    nb = Lq // Q_BLOCK
    scale = dk ** -0.5
    qb = jnp.moveaxis(q.reshape(B, nb, Q_BLOCK, Hkv, G, dk), 1, 0)
    sk = None if sink is None else sink.astype(jnp.float32).reshape(1, Hkv, G, 1, 1)

    def one_block(qi):
        s = jnp.einsum('bqhgd,bkhd->bhgqk', qi, k).astype(jnp.float32) * scale
        (p,) = joint_softmax([s], sk)
        return jnp.einsum('bhgqk,bkhd->bqhgd', p.astype(v.dtype), v)

    o = lax.map(one_block, qb)
    return jnp.moveaxis(o, 0, 1).reshape(B, Lq, Hq, dv)


def na_latent_attention(q, k, v, kc, vc, rpb):
    B, L, H, dh = q.shape
    rows = L // GRID_W
    wr = min(NA_WIN_ROWS, rows)
    nqb = GRID_W // NA_QCOLS
    scale = dh ** -0.5
    qcol = np.arange(GRID_W)
    col_start = np.clip(qcol - NA_WIN_COLS // 2, 0, GRID_W - NA_WIN_COLS)
    kc0 = np.minimum(col_start[::NA_QCOLS], GRID_W - NA_KCOLS)
    kcol = kc0[:, None] + np.arange(NA_KCOLS)[None, :]
    cs = col_start.reshape(nqb, NA_QCOLS)[:, :, None]
    col_valid = (kcol[:, None, :] >= cs) & (kcol[:, None, :] < cs + NA_WIN_COLS)
    dcol_idx = np.clip(kcol[:, None, :] - qcol.reshape(nqb, NA_QCOLS)[:, :, None] + NA_WIN_COLS - 1,
                       0, 2 * NA_WIN_COLS - 2)
    kg = k.reshape(B, rows, GRID_W, H, dh)
    vg = v.reshape(B, rows, GRID_W, H, dh)
    q_rows = jnp.moveaxis(q.reshape(B, rows, nqb, NA_QCOLS, H, dh), 1, 0)
    n_loc = wr * NA_KCOLS

    def one_row(args):
        r, qr = args
        rs = jnp.clip(r - wr // 2, 0, rows - wr)
        kb = lax.dynamic_slice_in_dim(kg, rs, wr, axis=1)[:, :, kcol]
        vb = lax.dynamic_slice_in_dim(vg, rs, wr, axis=1)[:, :, kcol]
        dr = rs + jnp.arange(wr) - r + (NA_WIN_ROWS - 1)
        bias = rpb[:, dr][:, :, dcol_idx].transpose(0, 2, 3, 1, 4)
        s_loc = jnp.einsum('bjqhd,brjkhd->bhjqrk', qr, kb).astype(jnp.float32) * scale
        s_loc = jnp.where(col_valid[:, :, None, :], s_loc + bias.astype(jnp.float32), NEG_INF)
        s_loc = s_loc.reshape(B, H, nqb, NA_QCOLS, n_loc)
        s_ctx = jnp.einsum('bjqhd,bchd->bhjqc', qr, kc).astype(jnp.float32) * scale
        p_loc, p_ctx = joint_softmax([s_loc, s_ctx])
        p_loc = p_loc.reshape(B, H, nqb, NA_QCOLS, wr, NA_KCOLS).astype(v.dtype)
        return (jnp.einsum('bhjqrk,brjkhd->bjqhd', p_loc, vb)
                + jnp.einsum('bhjqc,bchd->bjqhd', p_ctx.astype(v.dtype), vc))

    o = lax.map(one_row, (jnp.arange(rows), q_rows))
    return jnp.moveaxis(o, 0, 1).reshape(B, L, H, dh)


def swa_latent_attention(q, k, v, kc, vc, sink):
    B, L, Hq, dh = q.shape
    Hkv = k.shape[2]
    G = Hq // Hkv
    nb = L // Q_BLOCK
    span = 3 * Q_BLOCK
    scale = dh ** -0.5
    pad = ((0, 0), (Q_BLOCK, Q_BLOCK), (0, 0), (0, 0))
    kp, vp = jnp.pad(k, pad), jnp.pad(v, pad)
    qb = jnp.moveaxis(q.reshape(B, nb, Q_BLOCK, Hkv, G, dh), 1, 0)
    sk = sink.astype(jnp.float32).reshape(1, Hkv, G, 1, 1)

    def one_block(args):
        j, qj = args
        start = j * Q_BLOCK
        kj = lax.dynamic_slice_in_dim(kp, start, span, axis=1)
        vj = lax.dynamic_slice_in_dim(vp, start, span, axis=1)
        kpos = start - Q_BLOCK + jnp.arange(span)
        qpos = start + jnp.arange(Q_BLOCK)
        valid = ((kpos >= 0) & (kpos < L))[None, :] & (jnp.abs(kpos[None, :] - qpos[:, None]) <= SWA_WINDOW)
        s_loc = jnp.einsum('bqhgd,bkhd->bhgqk', qj, kj).astype(jnp.float32) * scale
        s_loc = jnp.where(valid, s_loc, NEG_INF)
        s_ctx = jnp.einsum('bqhgd,bchd->bhgqc', qj, kc).astype(jnp.float32) * scale
        p_loc, p_ctx = joint_softmax([s_loc, s_ctx], sk)
        return (jnp.einsum('bhgqk,bkhd->bqhgd', p_loc.astype(v.dtype), vj)
                + jnp.einsum('bhgqc,bchd->bqhgd', p_ctx.astype(v.dtype), vc))

    o = lax.map(one_block, (jnp.arange(nb), qb))
    return jnp.moveaxis(o, 0, 1).reshape(B, L, Hq, dh)


def mha_project(h, w_qkv, q_g, k_g, n_kv):
    B, L, _ = h.shape
    nq, nk = N_HEADS * HEAD_DIM, n_kv * HEAD_DIM
    qkv = h @ w_qkv
    q = qkv[..., :nq].reshape(B, L, N_HEADS, HEAD_DIM)
    k = qkv[..., nq:nq + nk].reshape(B, L, n_kv, HEAD_DIM)
    v = qkv[..., nq + nk:].reshape(B, L, n_kv, HEAD_DIM)
    return rms_norm(q, q_g), rms_norm(k, k_g), v


def mla_queries(h, w_dq, q_lora_g, w_uq, q_nope_g, q_pe_g):
    B, L, _ = h.shape
    q = (rms_norm(h @ w_dq, q_lora_g) @ w_uq).reshape(B, L, N_HEADS, MLA_NOPE + MLA_ROPE)
    return rms_norm(q[..., :MLA_NOPE], q_nope_g), rms_norm(q[..., MLA_NOPE:], q_pe_g)


def mla_compress(h, w_dkv, kv_lora_g, k_pe_g):
    ckv = h @ w_dkv
    return rms_norm(ckv[..., :MLA_KV_LORA], kv_lora_g), rms_norm(ckv[..., MLA_KV_LORA:], k_pe_g)


def mla_expand(c_kv, k_pe, w_ukv, k_nope_g):
    B, L, _ = c_kv.shape
    kv = (c_kv @ w_ukv).reshape(B, L, N_HEADS, MLA_NOPE + MLA_V)
    k_nope = rms_norm(kv[..., :MLA_NOPE], k_nope_g)
    k = jnp.concatenate([k_nope, jnp.broadcast_to(k_pe, (B, L, N_HEADS, MLA_ROPE)).astype(k_nope.dtype)], axis=-1)
    return k, kv[..., MLA_NOPE:]


def heads_out(o, w_o):
    return o.reshape(o.shape[0], o.shape[1], -1) @ w_o


def setup_inputs(seed: int = 0) -> dict:
    key = jax.random.key(seed)
    keys = iter(jax.random.split(key, 64))

    def normal(shape, scale):
        return jax.random.normal(next(keys), shape, dtype=jnp.float32) * scale

    def gain(shape):
        return 1.0 + normal(shape, 0.05)

    D, R = D_MODEL, N_REPEAT
    hq, hkv = N_HEADS * HEAD_DIM, N_KV_HEADS * HEAD_DIM
    return {
        'x_prompt': normal((BATCH, SEQ, D), 1.0),
        'x_sample': normal((DEC_BATCH, DEC_SEQ, D), 1.0),
        'cache_na_k': normal((DEC_BATCH, R, PAST_LEN, N_HEADS, HEAD_DIM), 1.0),
        'cache_na_v': normal((DEC_BATCH, R, PAST_LEN, N_HEADS, HEAD_DIM), 1.0),
        'cache_swa_k': normal((DEC_BATCH, R, PAST_LEN, N_KV_HEADS, HEAD_DIM), 1.0),
        'cache_swa_v': normal((DEC_BATCH, R, PAST_LEN, N_KV_HEADS, HEAD_DIM), 1.0),
        'cache_mla_ckv': normal((DEC_BATCH, R, PAST_LEN, MLA_KV_LORA), 1.0),
        'cache_mla_kpe': normal((DEC_BATCH, R, PAST_LEN, MLA_ROPE), 1.0),
        'cache_gqa_k': normal((DEC_BATCH, R, PAST_LEN, N_KV_HEADS, HEAD_DIM), 1.0),
        'cache_gqa_v': normal((DEC_BATCH, R, PAST_LEN, N_KV_HEADS, HEAD_DIM), 1.0),
        'c': normal((DEC_BATCH, D), 1.0),
        'c_ctx': normal((D,), 1.0),
        'norm1_g': gain((DEPTH, D)),
        'norm2_g': gain((DEPTH, D)),
        'w_mod': normal((DEPTH, D, 6 * D), 0.5 * D ** -0.5),
        'b_mod': normal((DEPTH, 6 * D), 0.02),
        'na_w_qkv': normal((R, D, 3 * hq), D ** -0.5),
        'na_q_g': gain((R, HEAD_DIM)),
        'na_k_g': gain((R, HEAD_DIM)),
        'na_rpb': normal((R, N_HEADS, 2 * NA_WIN_ROWS - 1, 2 * NA_WIN_COLS - 1), 0.1),
        'na_w_o': normal((R, hq, D), hq ** -0.5),
        'swa_w_qkv': normal((R, D, hq + 2 * hkv), D ** -0.5),
        'swa_q_g': gain((R, HEAD_DIM)),
        'swa_k_g': gain((R, HEAD_DIM)),
        'swa_sink': normal((R, N_HEADS), 0.5),
        'swa_w_o': normal((R, hq, D), hq ** -0.5),
        'mla_w_dq': normal((R, D, MLA_Q_LORA), D ** -0.5),
        'mla_q_lora_g': gain((R, MLA_Q_LORA)),
        'mla_w_uq': normal((R, MLA_Q_LORA, N_HEADS * (MLA_NOPE + MLA_ROPE)), MLA_Q_LORA ** -0.5),
        'mla_q_nope_g': gain((R, MLA_NOPE)),
        'mla_q_pe_g': gain((R, MLA_ROPE)),
        'mla_w_dkv': normal((R, D, MLA_KV_LORA + MLA_ROPE), D ** -0.5),
        'mla_kv_lora_g': gain((R, MLA_KV_LORA)),
        'mla_k_pe_g': gain((R, MLA_ROPE)),
        'mla_w_ukv': normal((R, MLA_KV_LORA, N_HEADS * (MLA_NOPE + MLA_V)), MLA_KV_LORA ** -0.5),
        'mla_k_nope_g': gain((R, MLA_NOPE)),
        'mla_w_o': normal((R, N_HEADS * MLA_V, D), (N_HEADS * MLA_V) ** -0.5),
        'gqa_w_qkv': normal((R, D, hq + 2 * hkv), D ** -0.5),
        'gqa_q_g': gain((R, HEAD_DIM)),
        'gqa_k_g': gain((R, HEAD_DIM)),
        'gqa_w_o': normal((R, hq, D), hq ** -0.5),
        'ffn_w_gate': normal((DEPTH, D, FFN_HIDDEN), D ** -0.5),
        'ffn_w_up': normal((DEPTH, D, FFN_HIDDEN), D ** -0.5),
        'ffn_w_down': normal((DEPTH, FFN_HIDDEN, D), FFN_HIDDEN ** -0.5),
    }


def reference(x_prompt, x_sample, cache_na_k, cache_na_v, cache_swa_k, cache_swa_v, cache_mla_ckv,
              cache_mla_kpe, cache_gqa_k, cache_gqa_v, c, c_ctx, norm1_g, norm2_g, w_mod, b_mod,
              na_w_qkv, na_q_g, na_k_g, na_rpb, na_w_o, swa_w_qkv, swa_q_g, swa_k_g, swa_sink, swa_w_o,
              mla_w_dq, mla_q_lora_g, mla_w_uq, mla_q_nope_g, mla_q_pe_g, mla_w_dkv, mla_kv_lora_g,
              mla_k_pe_g, mla_w_ukv, mla_k_nope_g, mla_w_o, gqa_w_qkv, gqa_q_g, gqa_k_g, gqa_w_o,
              ffn_w_gate, ffn_w_up, ffn_w_down):
    rows, cols = grid_positions(x_sample.shape[1])
    xp, xs = x_prompt, x_sample
    new_na_k, new_na_v, new_swa_k, new_swa_v = [], [], [], []
    new_mla_ckv, new_mla_kpe, new_gqa_k, new_gqa_v = [], [], [], []
    for li in range(DEPTH):
        kind, r = li % N_MIXERS, li // N_MIXERS
        mp = adaln(c_ctx[None, :], w_mod[li], b_mod[li])
        ms = adaln(c, w_mod[li], b_mod[li])
        hp = modulate(rms_norm(xp, norm1_g[li]), mp[0], mp[1])
        hs = modulate(rms_norm(xs, norm1_g[li]), ms[0], ms[1])
        if kind == 0:
            qp, kp_, vp_ = mha_project(hp, na_w_qkv[r], na_q_g[r], na_k_g[r], N_HEADS)
            op = heads_out(block_attention(qp, kp_, vp_), na_w_o[r])
            qs, ks_, vs_ = mha_project(hs, na_w_qkv[r], na_q_g[r], na_k_g[r], N_HEADS)
            os_ = heads_out(na_latent_attention(qs, ks_, vs_, cache_na_k[:, r], cache_na_v[:, r], na_rpb[r]), na_w_o[r])
            new_na_k.append(kp_)
            new_na_v.append(vp_)
        elif kind == 1:
            qp, kp_, vp_ = mha_project(hp, swa_w_qkv[r], swa_q_g[r], swa_k_g[r], N_KV_HEADS)
            op = heads_out(block_attention(qp, kp_, vp_, sink=swa_sink[r]), swa_w_o[r])
            qs, ks_, vs_ = mha_project(hs, swa_w_qkv[r], swa_q_g[r], swa_k_g[r], N_KV_HEADS)
            qs, ks_ = axial_rope(qs, rows, cols), axial_rope(ks_, rows, cols)
            os_ = heads_out(swa_latent_attention(qs, ks_, vs_, cache_swa_k[:, r], cache_swa_v[:, r], swa_sink[r]), swa_w_o[r])
            new_swa_k.append(kp_)
            new_swa_v.append(vp_)
        elif kind == 2:
            qn, qpe = mla_queries(hp, mla_w_dq[r], mla_q_lora_g[r], mla_w_uq[r], mla_q_nope_g[r], mla_q_pe_g[r])
            ckv, kpe = mla_compress(hp, mla_w_dkv[r], mla_kv_lora_g[r], mla_k_pe_g[r])
            k_ctx, v_ctx = mla_expand(ckv, kpe[:, :, None, :], mla_w_ukv[r], mla_k_nope_g[r])
            op = heads_out(block_attention(jnp.concatenate([qn, qpe], axis=-1), k_ctx, v_ctx), mla_w_o[r])
            qn_s, qpe_s = mla_queries(hs, mla_w_dq[r], mla_q_lora_g[r], mla_w_uq[r], mla_q_nope_g[r], mla_q_pe_g[r])
            q_s = jnp.concatenate([qn_s, axial_rope(qpe_s, rows, cols)], axis=-1)
            ckv_s, kpe_s = mla_compress(hs, mla_w_dkv[r], mla_kv_lora_g[r], mla_k_pe_g[r])
            k_lat, v_lat = mla_expand(ckv_s, axial_rope(kpe_s[:, :, None, :], rows, cols), mla_w_ukv[r], mla_k_nope_g[r])
            k_cc, v_cc = mla_expand(cache_mla_ckv[:, r], cache_mla_kpe[:, r][:, :, None, :], mla_w_ukv[r], mla_k_nope_g[r])
            os_ = heads_out(block_attention(q_s, jnp.concatenate([k_lat, k_cc], axis=1),
                                            jnp.concatenate([v_lat, v_cc], axis=1)), mla_w_o[r])
            new_mla_ckv.append(ckv)
            new_mla_kpe.append(kpe)
        else:
            qp, kp_, vp_ = mha_project(hp, gqa_w_qkv[r], gqa_q_g[r], gqa_k_g[r], N_KV_HEADS)
            op = heads_out(block_attention(qp, kp_, vp_), gqa_w_o[r])
            qs, ks_, vs_ = mha_project(hs, gqa_w_qkv[r], gqa_q_g[r], gqa_k_g[r], N_KV_HEADS)
            qs, ks_ = axial_rope(qs, rows, cols), axial_rope(ks_, rows, cols)
            os_ = heads_out(block_attention(qs, jnp.concatenate([ks_, cache_gqa_k[:, r]], axis=1),
                                            jnp.concatenate([vs_, cache_gqa_v[:, r]], axis=1)), gqa_w_o[r])
            new_gqa_k.append(kp_)
            new_gqa_v.append(vp_)
        xp = xp + mp[2] * op
        xs = xs + ms[2] * os_
        xp = xp + mp[5] * swiglu(modulate(rms_norm(xp, norm2_g[li]), mp[3], mp[4]), ffn_w_gate[li], ffn_w_up[li], ffn_w_down[li])
        xs = xs + ms[5] * swiglu(modulate(rms_norm(xs, norm2_g[li]), ms[3], ms[4]), ffn_w_gate[li], ffn_w_up[li], ffn_w_down[li])
    return (xp, xs, jnp.stack(new_na_k, axis=1), jnp.stack(new_na_v, axis=1), jnp.stack(new_swa_k, axis=1),
            jnp.stack(new_swa_v, axis=1), jnp.stack(new_mla_ckv, axis=1), jnp.stack(new_mla_kpe, axis=1),
            jnp.stack(new_gqa_k, axis=1), jnp.stack(new_gqa_v, axis=1))
```

```python
import contextlib
import numpy as np
import concourse.bass as bass
import concourse.mybir as mybir
from concourse.bass_utils import run_bass_kernel_spmd

F32 = mybir.dt.float32
BF16 = mybir.dt.bfloat16
AF = mybir.ActivationFunctionType
ALU = mybir.AluOpType

NCORES = 8
D = 1024
KC = 8
HID = 2816
HC = 22
EPS = 1e-6
NEG = -30000.0
TP = 1024
TS = 2048
LCTX = 512
ENGS = ("pe", "act", "dve", "pool", "sp")
NDMASEM = 12


class Op:
    __slots__ = ("eng", "fn", "deps", "signal", "count", "dma", "dsem", "dval", "idx", "cost", "fin", "seq")

    def __init__(self, eng, fn, dma):
        self.cost = 0.0
        self.fin = None
        self.seq = 0
        self.eng = eng
        self.fn = fn
        self.deps = []
        self.signal = False
        self.count = 0
        self.dma = dma
        self.dsem = None
        self.dval = 0
        self.idx = 0


class Prog:
    def __init__(self, nc):
        self.nc = nc
        self.q = {e: [] for e in ENGS}
        self.last_w = {}
        self.readers = {}
        self.ndma = {e: 0 for e in ENGS}
        self.out_dmas = []
        self.nops = 0

    def op(self, eng, fn, reads=(), writes=(), dma=False, is_out=False, cost=None, n=512):
        o = Op(eng, fn, dma)
        if cost is None:
            if dma:
                cost = 2500.0 + n / 0.15
            else:
                cost = {"pe": 30 + 0.43 * n, "act": 220 + 0.85 * n, "dve": 120 + 1.05 * n, "pool": 300 + 1.0 * n,
                        "sp": 50}[eng]
        o.cost = float(cost)
        o.seq = self.nops
        self.nops += 1
        deps = {}
        for k in reads:
            w = self.last_w.get(k)
            if w is not None:
                deps[id(w)] = w
        for k in writes:
            w = self.last_w.get(k)
            if w is not None:
                deps[id(w)] = w
            for r in self.readers.get(k, ()):
                deps[id(r)] = r
        o.deps = list(deps.values())
        for d in o.deps:
            d.signal = True
        for k in reads:
            self.readers.setdefault(k, []).append(o)
        for k in writes:
            self.last_w[k] = o
            self.readers[k] = []
        if dma:
            self.ndma[eng] += 1
            o.signal = True
            if is_out:
                self.out_dmas.append(o)
        self.q[eng].append(o)
        return o

    def dma(self, eng, out, in_, reads=(), writes=(), is_out=False, nbytes=262144):
        return self.op(eng, lambda e: e.dma_start(out=out, in_=in_), reads, writes, dma=True, is_out=is_out, n=nbytes)

    def schedule(self, window=None):
        window = window or {"pe": 24, "act": 16, "dve": 16, "pool": 8, "sp": 8}
        INF = float("inf")
        qs = {e: list(self.q[e]) for e in ENGS}
        head = {e: 0 for e in ENGS}
        issued = {e: [] for e in ENGS}
        free = {e: 0.0 for e in ENGS}
        taken = set()
        remaining = sum(len(v) for v in qs.values())
        dma_free = 0.0
        while remaining:
            best = None
            for e in ENGS:
                q = qs[e]
                h = head[e]
                while h < len(q) and id(q[h]) in taken:
                    h += 1
                head[e] = h
                cnt = 0
                i = h
                while i < len(q) and cnt < window[e]:
                    o = q[i]
                    i += 1
                    if id(o) in taken:
                        continue
                    cnt += 1
                    rdy = 0.0
                    ok = True
                    for d in o.deps:
                        if d.fin is None:
                            ok = False
                            break
                        if d.fin > rdy:
                            rdy = d.fin
                    if not ok:
                        continue
                    st = rdy if rdy > free[e] else free[e]
                    key = (st, o.seq)
                    if best is None or key < best[0]:
                        best = (key, e, o)
            assert best is not None
            (st, _), e, o = best
            if o.dma:
                issue = 80.0 if e == "sp" else 400.0
                free[e] = st + issue
                xfer = o.cost - 2500.0
                t0 = max(st + 1200.0, dma_free)
                dma_free = t0 + xfer
                o.fin = dma_free + 1300.0
            else:
                free[e] = st + o.cost
                o.fin = free[e] + 60.0
            taken.add(id(o))
            issued[e].append(o)
            remaining -= 1
        self.q = issued
        self.sim_time = max(free.values())

    def emit(self):
        nc = self.nc
        with contextlib.ExitStack() as es:
            esem = {e: es.enter_context(nc.semaphore("es_" + e)) for e in ENGS}
            dsem = {e: [es.enter_context(nc.semaphore(f"ds_{e}_{i}")) for i in range(NDMASEM)]
                    for e in ENGS if self.ndma[e] > 0}
            for e in ENGS:
                c = 0
                di = 0
                for o in self.q[e]:
                    if o.dma:
                        o.idx = di
                        di += 1
                        o.dsem = dsem[e][o.idx % NDMASEM]
                        o.dval = 16 * (o.idx // NDMASEM + 1)
                    elif o.signal:
                        c += 1
                        o.count = c
            block = es.enter_context(nc.Block())
            emap = {"pe": block.tensor, "act": block.scalar, "dve": block.vector, "pool": block.gpsimd,
                    "sp": block.sync}
            for e in ENGS:
                ops = self.q[e]
                final = self.out_dmas if e == "sp" else []

                def body(eng, ops=ops, e=e, final=final):
                    waited = {}

                    def wait(sem, val):
                        k = id(sem)
                        if waited.get(k, 0) >= val:
                            return
                        waited[k] = val
                        eng.wait_ge(sem, val)

                    for o in ops:
                        for d in o.deps:
                            if d.dma:
                                wait(d.dsem, d.dval)
                            else:
                                wait(esem[d.eng], d.count)
                        if o.dma and o.dval > 16:
                            wait(o.dsem, o.dval - 16)
                        ins = o.fn(eng)
                        if o.dma:
                            ins.then_inc(o.dsem, 16)
                        elif o.signal:
                            ins.then_inc(esem[e], 1)
                    for o in final:
                        wait(o.dsem, o.dval)

                emap[e](body)


class Rot:
    def __init__(self, name, tiles, keys=None):
        self.name = name
        self.tiles = tiles
        self.keys = keys if keys is not None else [(name, i) for i in range(len(tiles))]
        self.i = 0

    def next(self):
        i = self.i % len(self.tiles)
        self.i += 1
        return self.tiles[i], self.keys[i]


NA_TINT_LO, NA_TINT_HI = -11, 13
NA_TFULL_LO, NA_TFULL_HI = -7, 8
NA_NINT = NA_TINT_HI - NA_TINT_LO + 1
NA_NFULL = NA_TFULL_HI - NA_TFULL_LO + 1
NA_NBLK = NA_NINT + NA_NFULL


def _na_index_tables():
    kc = np.arange(64)[:, None]
    qc = np.arange(64)[None, :]
    cs = np.clip(qc - 8, 0, 48)
    cvalid = (kc >= cs) & (kc < cs + 16)
    dc = np.clip(kc - qc + 15, 0, 30)
    dr_idx = np.zeros((NA_NBLK, 128, 64), np.int64)
    dc_idx = np.zeros((NA_NBLK, 128, 64), np.int64)
    valid = np.zeros((NA_NBLK, 128, 64), bool)
    for b in range(NA_NBLK):
        if b < NA_NINT:
            delta = NA_TINT_LO + b
            interior = True
        else:
            delta = NA_TFULL_LO + (b - NA_NINT)
            interior = False
        for a in range(2):
            dr = a - delta + 7
            if interior:
                rv = -4 <= (a - delta) <= 3
            else:
                rv = 0 <= dr <= 14
            sl = slice(a * 64, (a + 1) * 64)
            dr_idx[b, sl] = min(max(dr, 0), 14)
            dc_idx[b, sl] = dc
            valid[b, sl] = cvalid & rv
    return dr_idx, dc_idx, valid


def _na_rs(qr):
    return min(max(qr - 4, 0), 24)


def _na_plan():
    plan = []
    for w in range(4):
        rows = list(range(8 * w, 8 * w + 8))
        lst = []
        for u in range(16):
            vr = [qr for qr in rows if any(_na_rs(qr) <= 2 * u + a < _na_rs(qr) + 8 for a in range(2))]
            if not vr:
                continue
            lo, hi = min(vr), max(vr) + 1
            runs = []
            for qr in range(lo, hi):
                delta = qr - 2 * u
                if qr < 4 and u <= 3:
                    blk = NA_NINT + (delta - NA_TFULL_LO)
                elif qr > 28 and u >= 12:
                    blk = NA_NINT + (delta - NA_TFULL_LO)
                else:
                    assert NA_TINT_LO <= delta <= NA_TINT_HI
                    blk = delta - NA_TINT_LO
                if runs and runs[-1][2] + runs[-1][1] == blk:
                    runs[-1][1] += 1
                else:
                    runs.append([qr - 8 * w, 1, blk])
            lst.append((u, lo - 8 * w, hi - 8 * w, [tuple(r) for r in runs]))
        plan.append(lst)
    return plan


NA_PLAN = _na_plan()


def _swa_table():
    ki = np.arange(128)[:, None]
    qi = np.arange(128)[None, :]
    t = np.full((128, 11, 128), NEG, np.float32)
    t[:, 4, :] = np.where(ki <= qi, 0.0, NEG)
    t[:, 5, :] = 0.0
    t[:, 6, :] = np.where(ki >= qi, 0.0, NEG)
    return t.reshape(128, 11 * 128)


def _rope_tables():
    t = np.arange(TS)
    rows = (t // 64).astype(np.float64)
    cols = (t % 64).astype(np.float64)
    cos64 = np.zeros((128, TS)); sin64 = np.zeros((128, TS)); perm64 = np.zeros((128, 128))
    inv16 = 10000.0 ** (-np.arange(16) / 16.0)
    for p in range(128):
        d = p % 64
        pos = rows if d < 32 else cols
        dd = d % 32
        f = dd % 16
        cos64[p] = np.cos(pos.astype(np.float32) * np.float32(inv16[f]))
        sin64[p] = np.sin(pos.astype(np.float32) * np.float32(inv16[f]))
        if dd < 16:
            perm64[p + 16, p] = -1.0
        else:
            perm64[p - 16, p] = 1.0
    cos96 = np.ones((128, TS)); sin96 = np.zeros((128, TS)); perm96 = np.zeros((128, 128))
    inv8 = 10000.0 ** (-np.arange(8) / 8.0)
    for p in range(64, 96):
        d = p - 64
        pos = rows if d < 16 else cols
        dd = d % 16
        f = dd % 8
        cos96[p] = np.cos(pos.astype(np.float32) * np.float32(inv8[f]))
        sin96[p] = np.sin(pos.astype(np.float32) * np.float32(inv8[f]))
        if dd < 8:
            perm96[p + 8, p] = -1.0
        else:
            perm96[p - 8, p] = 1.0
    return (cos64.astype(np.float32), sin64.astype(np.float32), perm64.astype(np.float32),
            cos96.astype(np.float32), sin96.astype(np.float32), perm96.astype(np.float32))


CM_IDENT, CM_B64, CM_B96, CM_O1024, CM_O256, CM_O128, CM_P64, CM_P96 = range(8)
NCM = 8


def _const_mats(perm64, perm96):
    cm = np.zeros((128, NCM, 128), np.float32)
    cm[:, CM_IDENT, :] = np.eye(128)
    b = np.zeros((128, 128)); b[0:64, 0:64] = 1 / 64.0; b[64:128, 64:128] = 1 / 64.0
    cm[:, CM_B64, :] = b
    b = np.zeros((128, 128)); b[0:64, 0:64] = 1 / 64.0; b[64:96, 64:96] = 1 / 32.0
    cm[:, CM_B96, :] = b
    cm[:, CM_O1024, :] = 1 / 1024.0
    cm[:, CM_O256, :] = 1 / 256.0
    cm[:, CM_O128, :] = 1 / 128.0
    cm[:, CM_P64, :] = perm64
    cm[:, CM_P96, :] = perm96
    return cm.reshape(128, NCM * 128)


SP_G1 = 0
SP_G2 = 32
SP_BM = 64
SP_NAQ, SP_NAK, SP_SWQ, SP_SWK, SP_GQQ, SP_GQK = 256, 257, 258, 259, 260, 261
SP_QLORA = 262
SP_KVLORA = 264
SP_Q96 = 265
SP_K96 = 266
SP_SINK = 267
SP_EPS = 283
SP_ZERO = 284
NSP = 285


def _tile64(g):
    return np.concatenate([g, g]).astype(np.float32)


def _small_params(inp):
    sp = np.zeros((128, NSP), np.float32)
    for li in range(4):
        sp[:, SP_G1 + li * 8: SP_G1 + li * 8 + 8] = inp["norm1_g"][li].reshape(8, 128).T
        sp[:, SP_G2 + li * 8: SP_G2 + li * 8 + 8] = inp["norm2_g"][li].reshape(8, 128).T
        sp[:, SP_BM + li * 48: SP_BM + li * 48 + 48] = inp["b_mod"][li].reshape(48, 128).T
    sp[:, SP_NAQ] = _tile64(inp["na_q_g"][0]); sp[:, SP_NAK] = _tile64(inp["na_k_g"][0])
    sp[:, SP_SWQ] = _tile64(inp["swa_q_g"][0]); sp[:, SP_SWK] = _tile64(inp["swa_k_g"][0])
    sp[:, SP_GQQ] = _tile64(inp["gqa_q_g"][0]); sp[:, SP_GQK] = _tile64(inp["gqa_k_g"][0])
    sp[:, SP_QLORA: SP_QLORA + 2] = inp["mla_q_lora_g"][0].reshape(2, 128).T
    sp[:, SP_KVLORA] = inp["mla_kv_lora_g"][0]
    sp[0:64, SP_Q96] = inp["mla_q_nope_g"][0]; sp[64:96, SP_Q96] = inp["mla_q_pe_g"][0]
    sp[0:64, SP_K96] = inp["mla_k_nope_g"][0]; sp[64:96, SP_K96] = inp["mla_k_pe_g"][0]
    sp[:, SP_SINK: SP_SINK + 16] = inp["swa_sink"][0][None, :]
    sp[:, SP_EPS] = EPS
    return sp


W_SHAPES = {
    "na_w_qkv": [1024, 3072], "na_w_o": [1024, 1024],
    "swa_w_qkv": [1024, 1536], "swa_w_o": [1024, 1024],
    "mla_w_dq": [1024, 256], "mla_w_uq": [256, 1536], "mla_w_dkv": [1024, 160],
    "mla_w_ukv": [128, 2048], "mla_w_o": [1024, 1024],
    "gqa_w_qkv": [1024, 1536], "gqa_w_o": [1024, 1024],
}
WMB = 384
for _li in range(4):
    W_SHAPES[f"w_modr{_li}"] = [6144 // WMB, 128, KC * WMB]
    W_SHAPES[f"ffn_gu{_li}"] = [HC, 128, 2 * KC * 128]
    W_SHAPES[f"ffn_d{_li}"] = [KC, 128, HC * 128]
W_SHAPES.update({"na_wq": [8, 128, KC * 128], "na_wk": [8, 128, KC * 128], "na_wv": [8, 128, KC * 128],
                 "swa_wq": [4, 128, KC * 256], "swa_wk": [4, 128, KC * 128], "swa_wv": [4, 128, KC * 64],
                 "gqa_wq": [4, 128, KC * 256], "gqa_wk": [4, 128, KC * 128], "gqa_wv": [4, 128, KC * 64],
                 "mla_wdq": [128, KC * 256], "mla_wckv": [128, KC * 128]})
SKIP = set()
SCHED = True
IN_SHAPES = {
    "xpT": [1024, TP], "xsT": [1024, TS], "cv": [128, 16],
    "na_kcT": [1024, LCTX], "na_vc": [LCTX, 1024],
    "swa_kcT": [256, LCTX], "swa_vc": [LCTX, 256],
    "mla_ckvT": [128, LCTX], "mla_kpeT": [32, LCTX],
    "gqa_kcT": [256, LCTX], "gqa_vc": [LCTX, 256],
    "spm": [128, NSP], "cmat": [128, NCM * 128],
    "natab": [16, 128, NA_NBLK * 64], "swatab": [128, 11 * 128],
    "rope64": [2, 128, TS], "rope96": [2, 128, TS],
}
OUT_SHAPES = {
    "ypT": [1024, TP], "ysT": [1024, TS],
    "o_na_kT": [1024, TP], "o_na_v": [TP, 1024],
    "o_swa_kT": [256, TP], "o_swa_v": [TP, 256],
    "o_mla_ckvT": [128, TP], "o_mla_kpeT": [32, TP],
    "o_gqa_kT": [256, TP], "o_gqa_v": [TP, 256],
}


class Region:
    def __init__(self, name, tensor, nbytes):
        self.name = name
        self.t = tensor
        self.nbytes = nbytes

    def view(self, off, shape, dtype):
        esz = 4 if dtype == F32 else 2
        n = int(np.prod(shape))
        assert off % 4 == 0 and off + n * esz <= self.nbytes, (self.name, off, shape)
        ap = self.t[:, off // 2: off // 2 + n * esz // 2]
        if dtype == F32:
            ap = ap.bitcast(F32)
        if len(shape) == 2:
            ap = ap.rearrange("p (a b) -> p a b", a=shape[0])
        elif len(shape) == 3:
            ap = ap.rearrange("p (a b c) -> p a b c", a=shape[0], b=shape[1])
        return ap

    def keys(self, off, nbytes):
        return [(self.name, s) for s in range(off // 1024, (off + nbytes - 1) // 1024 + 1)]


def build_program(nlayers=4, passes=(0, 1), dbg=False):
    nc = bass.Bass("TRN2", target_bir_lowering=False)
    class LazyDR(dict):
        def __missing__(self, n):
            shp = W_SHAPES[n] if n in W_SHAPES else IN_SHAPES[n]
            v = nc.dram_tensor(n, list(shp), F32, kind="ExternalInput").ap()
            self[n] = v
            return v
    dr = LazyDR()
    for n, s in OUT_SHAPES.items():
        dr[n] = nc.dram_tensor(n, list(s), F32, kind="ExternalOutput").ap()
    nc.used_inputs = dr

    with contextlib.ExitStack() as es:
        def sb(name, shape, dt):
            return es.enter_context(nc.sbuf_tensor("sb_" + name, list(shape), dt))

        P = Prog(nc)
        x = sb("x", [128, KC, TS], F32)
        hT = sb("hT", [128, KC, TS], BF16)
        U_BYTES = 65024
        U = Region("U", sb("U", [128, U_BYTES // 2], BF16), U_BYTES)
        TB_BYTES = 12288
        TB = Region("TB", sb("TB", [128, TB_BYTES // 2], BF16), TB_BYTES)
        swat = sb("swat", [128, 11 * 128], BF16)
        cm = sb("cm", [128, NCM, 128], BF16)
        spm = sb("spm", [128, NSP], F32)
        cvt = sb("cvt", [128, 16], F32)
        silc = sb("silc", [128, KC, 2], BF16)
        mod = sb("mod", [128, 4, 48, 2], F32)
        A1 = sb("A1", [128, 4, KC, 2], F32)
        A2 = sb("A2", [128, 4, KC, 2], F32)
        gs = sb("gs", [128, 8], F32)
        esink = sb("esink", [128, 16], F32)
        psbig = [es.enter_context(nc.psum_tensor(f"ps{i}", [128, 1024], F32)) for i in range(4)]
        psb = [psbig[i // 2][:, (i % 2) * 512:(i % 2 + 1) * 512] for i in range(8)]

        def rot(name, n, shape, dt):
            return Rot(name, [sb(f"{name}{i}", shape, dt) for i in range(n)])

        r_sq = rot("sq", 2, [128, 512], BF16)
        r_rstd = rot("rstd", 3, [128, 512], F32)
        r_qn = rot("qn", 2, [128, 512], BF16)
        r_nrm = rot("nrm", 2, [128, 512], F32)
        r_pT = rot("pT", 2, [128, 1024], BF16)
        r_rden = rot("rden", 2, [128, 512], F32)
        r_sg = rot("sg", 2, [128, 512], BF16)
        r_stage = rot("stage", 2, [128, 512], F32)
        r_t1 = Rot("stage", [r_stage.tiles[0]], [("stage", 0)])
        r_t2 = Rot("stage", [r_stage.tiles[1]], [("stage", 1)])

        class PsRot:
            def __init__(self, banks):
                self.banks = banks
                self.i = 0

            def next(self):
                b = self.banks[self.i % len(self.banks)]
                self.i += 1
                return psb[b], ("ps", b)

        psA = PsRot([0, 1, 4])
        psB = PsRot([2, 3, 5])
        psS = PsRot([4, 5, 6, 7])
        psW = PsRot([6, 7])
        psO = PsRot([0, 1])

        class PsPairRot:
            def __init__(self, pairs):
                self.pairs = pairs
                self.i = 0

            def next(self):
                b = self.pairs[self.i % len(self.pairs)]
                self.i += 1
                return psbig[b], [("ps", 2 * b), ("ps", 2 * b + 1)]
        psS2 = PsPairRot([1, 2, 3])

        eps_ap = spm[:, SP_EPS:SP_EPS + 1]
        zero_ap = spm[:, SP_ZERO:SP_ZERO + 1]

        O_QG, O_KG, O_VA = 0, 8192, 18432
        O_GW = [28672, 40960]
        qg = U.view(O_QG, [2, TS], BF16)
        kg = U.view(O_KG, [2, TS + LCTX], BF16)
        vaug = U.view(O_VA, [20, 2, 128], BF16)
        GW_WQ, GW_WK, GW_WV, GW_WO = 0, 4096, 6144, 8192

        def kq(c, t):
            return U.keys(O_QG + (c * TS + t * 512) * 2, 1024)

        def kk(c, t):
            return U.keys(O_KG + (c * (TS + LCTX) + t * 512) * 2, 1024)

        def kva(ch4):
            return U.keys(O_VA + ch4 * 2048, 2048)

        def kvr(c0, n):
            return U.keys(O_VA + c0 * 512, n * 512)

        O_QLAT, O_CKV = 40960, 49152
        O_MW_DQ, O_MW_CKV, O_MW_KPE = 54272, 58368, 60416
        O_HID, O_WGU, O_WD = 0, 45056, 53248
        hid = U.view(O_HID, [HC, 1024], BF16)

        def khid(j, tl):
            return U.keys(O_HID + (j * 1024 + tl * 512) * 2, 1024)

        def kx(c, t):
            return [("x", c, t)]

        def khT(t):
            return [("hT", t)]

        def wview(w2d, r0, nk, c0, ncols):
            return w2d[r0:r0 + 128 * nk, c0:c0 + ncols].rearrange("(k p) n -> p k n", p=128)

        P.dma("pool", cm[:], dr["cmat"].rearrange("p (a b) -> p a b", a=NCM), writes=["cm"])
        P.dma("sp", spm[:], dr["spm"], writes=["spm"])
        P.dma("sp", cvt[:], dr["cv"], writes=["cv"])
        P.dma("pool", swat[:], dr["swatab"], writes=["swat"])
        vaug3 = U.view(O_VA, [40, 128], BF16)

        def set_ones():
            P.op("dve", lambda e: e.memset(vaug3[:, :, 64:128], 1.0), writes=U.keys(O_VA, 10240))
        for i, (col, sc) in enumerate([(SP_NAQ, 0.125), (SP_SWQ, 0.125), (SP_GQQ, 0.125), (SP_Q96, 96.0 ** -0.5)]):
            P.op("dve", lambda e, i=i, col=col, sc=sc: e.tensor_scalar(out=gs[:, i:i + 1], in0=spm[:, col:col + 1],
                                                                  scalar1=float(sc), scalar2=None, op0=ALU.mult),
                 reads=["spm"], writes=[("gs", i)])
        GS = {"na": 0, "swa": 1, "gqa": 2, "mla": 3}
        P.op("act", lambda e: e.activation(out=esink[:], in_=spm[:, SP_SINK:SP_SINK + 16], func=AF.Exp),
             reads=["spm"], writes=["esink"])

        P.op("act", lambda e: e.activation(out=silc[:].rearrange("p a b -> p (a b)"), in_=cvt[:], func=AF.Silu),
             reads=["cv"], writes=["silc"])
        wmodb = [TB.view(0, [KC, WMB], BF16), TB.view(6144, [KC, WMB], BF16)]
        wmodf = [TB.view(0, [KC * WMB], BF16), TB.view(6144, [KC * WMB], BF16)]
        wmodk = [TB.keys(0, 6144), TB.keys(6144, 6144)]
        adaln_state = {"n": 0}

        def adaln_layer(li):
            for blk in range(6144 // WMB):
                b = adaln_state["n"] % 2
                adaln_state["n"] += 1
                P.dma("pool", wmodf[b], dr[f"w_modr{li}"][blk], writes=wmodk[b], nbytes=WMB * 4096)
                pt, pk = psA.next()
                nj = WMB // 128

                def mm(e, b=b, pt=pt):
                    for jj in range(nj):
                        for k in range(KC):
                            ins = e.matmul(pt[:, jj * 2:jj * 2 + 2], wmodb[b][:, k, jj * 128:(jj + 1) * 128], silc[:, k, :],
                                           start=(k == 0), stop=(k == KC - 1), skip_group_check=True)
                    return ins
                P.op("pe", mm, reads=wmodk[b] + ["silc"], writes=[pk], cost=nj * 8 * 75)
                j0 = blk * nj
                P.op("dve", lambda e, li=li, pt=pt, j0=j0: e.tensor_tensor(
                    out=mod[:, li, j0:j0 + nj, :], in0=pt[:, 0:2 * nj].rearrange("p (a b) -> p a b", b=2),
                    in1=spm[:, SP_BM + li * 48 + j0: SP_BM + li * 48 + j0 + nj].unsqueeze(2).broadcast_to([128, nj, 2]),
                    op=ALU.add), reads=[pk, "spm"], writes=[("mod", li)], n=64)
            for (Aout, v, gcol, nm) in ((A1, 1, SP_G1, "A1"), (A2, 4, SP_G2, "A2")):
                P.op("dve", lambda e, li=li, Aout=Aout, v=v, gcol=gcol: e.scalar_tensor_tensor(
                    out=Aout[:, li], in0=mod[:, li, v * 8:(v + 1) * 8, :], scalar=1.0,
                    in1=spm[:, gcol + li * 8: gcol + li * 8 + 8].unsqueeze(2).broadcast_to([128, 8, 2]),
                    op0=ALU.add, op1=ALU.mult), reads=[("mod", li), "spm"], writes=[(nm, li)], n=64)

        if nlayers > 0:
            adaln_layer(0)

        def modv(li, v, c, g):
            return mod[:, li, v * 8 + c, g:g + 1]

        def norm_tile(li, which, t, g):
            Am = A1 if which == 1 else A2
            vshift = 0 if which == 1 else 3
            cs = slice(t * 512, (t + 1) * 512)
            pt, pk = psB.next()
            xk = []
            for c in range(KC):
                xk += kx(c, t)
            P.op("act", lambda e: e.activation(out=hT[:, :, cs], in_=x[:, :, cs], func=AF.Square),
                 reads=xk, writes=khT(t), n=4096)

            def ssq(e, pt=pt):
                for c in range(KC):
                    ins = e.matmul(pt[:], cm[:, CM_O1024, :], hT[:, c, cs], start=(c == 0), stop=(c == KC - 1))
                return ins
            P.op("pe", ssq, reads=khT(t) + ["cm"], writes=[pk], cost=8 * 250)
            rs, rk = r_rstd.next()
            P.op("act", lambda e, rs=rs, pt=pt: e.activation(out=rs[:], in_=pt[:], func=AF.Ln, bias=eps_ap, scale=1.0),
                 reads=[pk, "spm"], writes=[rk])
            P.op("act", lambda e, rs=rs: e.activation(out=rs[:], in_=rs[:], func=AF.Exp, scale=-0.5), reads=[rk], writes=[rk])
            for c in range(KC):
                nt, nk = r_nrm.next()
                P.op("dve", lambda e, nt=nt, c=c, rs=rs: e.tensor_tensor(out=nt[:], in0=x[:, c, cs], in1=rs[:], op=ALU.mult),
                     reads=kx(c, t) + [rk], writes=[nk])
                P.op("act", lambda e, nt=nt, c=c: e.activation(out=hT[:, c, cs], in_=nt[:], func=AF.Identity,
                                                              bias=modv(li, vshift, c, g), scale=Am[:, li, c, g:g + 1]),
                     reads=[nk, ("mod", li), ("A1" if which == 1 else "A2", li)], writes=khT(t))

        def proj_fm(M, nk, lhs_fn, rhs_fn, rhs_keys_fn, wkeys, tiles, normmat, gain_ap, gain_keys, dests_fn,
                    rope=None, kout=None, r0=0):
            if "pfm" in SKIP:
                return
            for t in tiles:
                pa, pak = psA.next()

                lhs_l = [lhs_fn(k) for k in range(nk)]
                rhs_l = [rhs_fn(k, t) for k in range(nk)]

                def mm(e, pa=pa, lhs_l=lhs_l, rhs_l=rhs_l):
                    for k in range(nk):
                        ins = e.matmul(pa[0:M, :], lhs_l[k], rhs_l[k], start=(k == 0), stop=(k == nk - 1))
                    return ins
                P.op("pe", mm, reads=wkeys + rhs_keys_fn(t), writes=[pak], cost=nk * 250)
                sq, sk = r_sq.next()
                P.op("act", lambda e, sq=sq, pa=pa: e.activation(out=sq[0:M, :], in_=pa[0:M, :], func=AF.Square),
                     reads=[pak], writes=[sk])
                pb, pbk = psB.next()
                P.op("pe", lambda e, pb=pb, sq=sq: e.matmul(pb[0:M, :], normmat, sq[0:M, :], start=True, stop=True),
                     reads=[sk, "cm"], writes=[pbk])
                rs, rk = r_rstd.next()
                P.op("act", lambda e, rs=rs, pb=pb: e.activation(out=rs[0:M, :], in_=pb[0:M, :], func=AF.Ln,
                                                              bias=eps_ap[0:M], scale=1.0),
                     reads=[pbk, "spm"], writes=[rk])
                P.op("act", lambda e, rs=rs: e.activation(out=rs[0:M, :], in_=rs[0:M, :], func=AF.Exp, scale=-0.5),
                     reads=[rk], writes=[rk])
                dests = dests_fn(t)
                if rope is None and kout is None:
                    for (dap, dk, rsl) in dests:
                        P.op("dve", lambda e, dap=dap, rsl=rsl, pa=pa, rs=rs: e.scalar_tensor_tensor(
                            out=dap, in0=pa[rsl, :], scalar=gain_ap[rsl], in1=rs[rsl, :], op0=ALU.mult, op1=ALU.mult),
                            reads=[pak, rk] + gain_keys, writes=dk)
                elif rope is None:
                    st, stk = r_stage.next()
                    P.op("dve", lambda e, st=st, pa=pa, rs=rs: e.scalar_tensor_tensor(
                        out=st[0:M, :], in0=pa[0:M, :], scalar=gain_ap[0:M], in1=rs[0:M, :], op0=ALU.mult, op1=ALU.mult),
                        reads=[pak, rk] + gain_keys, writes=[stk])
                    for (dap, dk, rsl) in dests:
                        P.op("act", lambda e, dap=dap, rsl=rsl, st=st: e.activation(out=dap, in_=st[rsl, :], func=AF.Copy),
                             reads=[stk], writes=dk)
                    kap, krows = kout(t)
                    P.dma("sp", kap, st[krows, :], reads=[stk], is_out=True)
                else:
                    perm, cosf, sinf, ropek = rope
                    rkeys = ropek(t)
                    cos_ap = cosf(t)
                    sin_ap = sinf(t)
                    qn, qk = r_qn.next()
                    P.op("dve", lambda e, qn=qn, pa=pa, rs=rs: e.scalar_tensor_tensor(
                        out=qn[0:M, :], in0=pa[0:M, :], scalar=gain_ap[0:M], in1=rs[0:M, :], op0=ALU.mult, op1=ALU.mult),
                        reads=[pak, rk] + gain_keys, writes=[qk])
                    pc, pck = psB.next()
                    P.op("pe", lambda e, pc=pc, qn=qn: e.matmul(pc[0:M, :], perm, qn[0:M, :], start=True, stop=True),
                         reads=[qk, "cm"], writes=[pck])
                    t1, t1k = r_t1.next()
                    t2, t2k = r_t2.next()
                    P.op("dve", lambda e, t1=t1, qn=qn, cos_ap=cos_ap: e.tensor_tensor(out=t1[0:M, :], in0=qn[0:M, :],
                                                                                  in1=cos_ap[0:M, :], op=ALU.mult),
                         reads=[qk] + rkeys, writes=[t1k])
                    P.op("dve", lambda e, t2=t2, pc=pc, sin_ap=sin_ap: e.tensor_tensor(out=t2[0:M, :], in0=pc[0:M, :],
                                                                                  in1=sin_ap[0:M, :], op=ALU.mult),
                         reads=[pck] + rkeys, writes=[t2k])
                    for (dap, dk, rsl) in dests:
                        P.op("dve", lambda e, dap=dap, rsl=rsl, t1=t1, t2=t2: e.tensor_tensor(out=dap, in0=t1[rsl, :], in1=t2[rsl, :],
                                                                                      op=ALU.add),
                             reads=[t1k, t2k], writes=dk)

        def proj_tm(N, nk, lhs_fn, lhs_keys_fn, rhs_fn, wkeys, chunks, dest_fn, vout=None):
            per = 512 // N
            i = 0
            if "ptm" in SKIP:
                return
            while i < len(chunks):
                grp = chunks[i:i + per]
                assert grp == list(range(grp[0], grp[0] + len(grp)))
                i += per
                pa, pak = psA.next()

                lhs_l = [[lhs_fn(k, ch) for k in range(nk)] for ch in grp]
                rhs_l = [rhs_fn(k) for k in range(nk)]

                def mm(e, pa=pa, lhs_l=lhs_l, rhs_l=rhs_l):
                    for gi in range(len(lhs_l)):
                        for k in range(nk):
                            ins = e.matmul(pa[:, gi * N:(gi + 1) * N], lhs_l[gi][k], rhs_l[k], start=(k == 0),
                                           stop=(k == nk - 1), skip_group_check=True)
                    return ins
                rk = []
                for ch in grp:
                    rk += lhs_keys_fn(ch)
                P.op("pe", mm, reads=wkeys + rk, writes=[pak], cost=len(grp) * nk * (30 + 0.43 * N))
                n = len(grp)
                dap, dk = dest_fn(grp[0], n)
                if len(dap.shape) == 3:
                    P.op("act", lambda e, dap=dap, pa=pa, n=n: e.activation(out=dap, in_=pa[:, 0:n * N].rearrange(
                        "p (a b) -> p a b", a=n), func=AF.Copy), reads=[pak], writes=dk + [pak])
                else:
                    for hh in range(dap.shape[2]):
                        P.op("act", lambda e, dap=dap, pa=pa, n=n, hh=hh: e.activation(
                            out=dap[:, :, hh, :], in_=pa[:, 0:n * N].rearrange("p (a h b) -> p a h b", a=n, b=64)[:, :, hh, :],
                            func=AF.Copy), reads=[pak], writes=dk + [pak])
                if vout is not None and "vout" not in SKIP:
                    st, stk = r_stage.next()
                    P.op("dve", lambda e, st=st, pa=pa, n=n: e.tensor_copy(out=st[:, 0:n * N], in_=pa[:, 0:n * N]),
                         reads=[pak], writes=[stk, pak])
                    vo = vout(grp[0], n)
                    for a_ in range(n):
                        P.dma("sp", vo[:, a_, :], st[:, a_ * N:(a_ + 1) * N], reads=[stk], is_out=True)

        def attend(K, krow0, qc, kc_slot, vslot, q0, nq, chunks, out_ap, out_keys, sink_ap=None):
            if "attn" in SKIP:
                return
            po, pok = psO.next()
            rows = slice(krow0, krow0 + K)
            batches = []
            i = 0
            while i < len(chunks):
                c = chunks[i]
                w = c[3] - c[2]
                if (i + 1 < len(chunks) and not c[4] and not chunks[i + 1][4] and (c[2], c[3]) == (chunks[i + 1][2], chunks[i + 1][3])
                        and (w == 512 or w <= 256)):
                    batches.append([c, chunks[i + 1]])
                    i += 2
                else:
                    batches.append([c])
                    i += 1
            nb_ = len(batches)
            for bi, bat in enumerate(batches):
                lo, hi = bat[0][2], bat[0][3]
                w = hi - lo
                if len(bat) == 2:
                    ps_, psk = psS2.next() if w == 512 else psS.next()
                    if w != 512:
                        psk = [psk]
                    offs = [0, w]
                else:
                    ps_, psk1 = psS.next()
                    psk = [psk1]
                    offs = [lo]
                tot = w * len(bat)
                base = offs[0]

                def mm(e, ps_=ps_, bat=bat, offs=offs, lo=lo, hi=hi, w=w):
                    for j, (kcol0, vch, _, _, runs) in enumerate(bat):
                        ins = e.matmul(ps_[:, offs[j]:offs[j] + w], kg[rows, kc_slot, kcol0:kcol0 + 128],
                                       qg[rows, qc, q0 + lo:q0 + hi], start=True, stop=(len(runs) == 0), skip_group_check=True)
                        for ri, (c0, ncol, rap, _) in enumerate(runs):
                            ins = e.matmul(ps_[:, c0:c0 + ncol], cm[:, CM_IDENT, :], rap, start=False,
                                           stop=(ri == len(runs) - 1), skip_group_check=True)
                    return ins
                rk = kq(qc, q0 // 512) + ["cm"]
                cost = 0.0
                for (kcol0, vch, _, _, runs) in bat:
                    rk += kk(kc_slot, kcol0 // 512)
                    cost += (1 + len(runs)) * (30 + 0.43 * w)
                    for r_ in runs:
                        rk += r_[3]
                P.op("pe", mm, reads=rk, writes=psk, cost=cost)
                pt, ptk = r_pT.next()
                P.op("act", lambda e, pt=pt, ps_=ps_, base=base, tot=tot: e.activation(
                    out=pt[:, base:base + tot], in_=ps_[:, base:base + tot], func=AF.Exp),
                    reads=psk, writes=[ptk], n=tot)

                def pv(e, po=po, pt=pt, bat=bat, offs=offs, lo=lo, hi=hi, w=w, bi=bi):
                    for j, (kcol0, vch, _, _, runs) in enumerate(bat):
                        ins = e.matmul(po[:, lo:hi], vaug[:, vch, vslot, :], pt[:, offs[j]:offs[j] + w],
                                       start=(bi == 0 and j == 0), stop=(bi == nb_ - 1 and j == len(bat) - 1),
                                       skip_group_check=True)
                    return ins
                vk = []
                for (kcol0, vch, _, _, runs) in bat:
                    vk += kva(vch // 4)
                P.op("pe", pv, reads=[ptk] + vk, writes=[pok], cost=len(bat) * (30 + 0.43 * w))
            rd, rdk = r_rden.next()
            bias_ = sink_ap[64:128] if sink_ap is not None else zero_ap[64:128]
            P.op("act", lambda e, rd=rd, po=po, bias_=bias_: e.activation(out=rd[64:128, 0:nq], in_=po[64:128, 0:nq], func=AF.Ln,
                                                                   bias=bias_, scale=1.0),
                 reads=[pok, "esink", "spm"], writes=[rdk], n=nq)
            P.op("act", lambda e, rd=rd: e.activation(out=rd[64:128, 0:nq], in_=rd[64:128, 0:nq], func=AF.Exp, scale=-1.0),
                 reads=[rdk], writes=[rdk], n=nq)
            P.op("dve", lambda e, rd=rd, po=po: e.tensor_tensor(out=out_ap, in0=po[0:64, 0:nq], in1=rd[64:128, 0:nq], op=ALU.mult),
                 reads=[pok, rdk], writes=out_keys)

        def wo_group(li, g, wo_ap, wokeys, nkc, T):
            if "wo" in SKIP:
                return
            for t in range(T // 512):
                for c in range(KC):
                    pa, pak = psW.next()

                    def mm(e, pa=pa, c=c, t=t):
                        for k in range(nkc):
                            ins = e.matmul(pa[:], wo_ap[:, k, c * 128:(c + 1) * 128], qg[:, k, t * 512:(t + 1) * 512],
                                           start=(k == 0), stop=(k == nkc - 1))
                        return ins
                    rk = []
                    for k in range(nkc):
                        rk += kq(k, t)
                    P.op("pe", mm, reads=wokeys + rk, writes=[pak], cost=nkc * 250)
                    P.op("dve", lambda e, pa=pa, c=c, t=t: e.scalar_tensor_tensor(
                        out=x[:, c, t * 512:(t + 1) * 512], in0=pa[:], scalar=modv(li, 2, c, g),
                        in1=x[:, c, t * 512:(t + 1) * 512], op0=ALU.mult, op1=ALU.add),
                        reads=[pak, ("mod", li)] + kx(c, t), writes=kx(c, t))

        def ffn(li, g, T):
            wgu, wd = dr[f"ffn_gu{li}"], dr[f"ffn_d{li}"]
            nb = [0, 0]
            for s in range(T // 1024):
                for tl in range(2):
                    norm_tile(li, 2, s * 2 + tl, g)
                for j in range(HC):
                    b = nb[0] % 2
                    nb[0] += 1
                    wb = U.view(O_WGU + b * 4096, [2, KC, 128], BF16)
                    wbk = U.keys(O_WGU + b * 4096, 4096)
                    P.dma("pool", U.view(O_WGU + b * 4096, [2 * KC * 128], BF16), wgu[j], writes=wbk, nbytes=1 << 20)
                    for tl in range(2):
                        t = s * 2 + tl
                        cs = slice(t * 512, (t + 1) * 512)
                        pg, pgk = psA.next()
                        pu, puk = psB.next()

                        def mm(e, pg=pg, pu=pu, wb=wb, cs=cs):
                            for k in range(KC):
                                e.matmul(pg[:], wb[:, 0, k, :], hT[:, k, cs], start=(k == 0), stop=(k == KC - 1))
                            for k in range(KC):
                                ins = e.matmul(pu[:], wb[:, 1, k, :], hT[:, k, cs], start=(k == 0), stop=(k == KC - 1))
                            return ins
                        P.op("pe", mm, reads=wbk + khT(t), writes=[pgk, puk], cost=16 * 250)
                        sg, sgk = r_sg.next()
                        P.op("act", lambda e, sg=sg, pg=pg: e.activation(out=sg[:], in_=pg[:], func=AF.Silu),
                             reads=[pgk], writes=[sgk])
                        P.op("dve", lambda e, sg=sg, pu=pu, j=j, tl=tl: e.tensor_tensor(
                            out=hid[:, j, tl * 512:(tl + 1) * 512], in0=pu[:], in1=sg[:], op=ALU.mult),
                            reads=[puk, sgk], writes=khid(j, tl))
                for c in range(KC):
                    b = nb[1] % 2
                    nb[1] += 1
                    wdb = U.view(O_WD + b * 6144, [HC, 128], BF16)
                    wdk = U.keys(O_WD + b * 6144, 5632)
                    P.dma("pool", U.view(O_WD + b * 6144, [HC * 128], BF16), wd[c], writes=wdk, nbytes=1408 << 10)
                    for tl in range(2):
                        t = s * 2 + tl
                        pd, pdk = psS.next()

                        def mm(e, pd=pd, wdb=wdb, tl=tl):
                            for j in range(HC):
                                ins = e.matmul(pd[:], wdb[:, j, :], hid[:, j, tl * 512:(tl + 1) * 512], start=(j == 0),
                                               stop=(j == HC - 1))
                            return ins
                        rk = []
                        for j in range(HC):
                            rk += khid(j, tl)
                        P.op("pe", mm, reads=wdk + rk, writes=[pdk], cost=22 * 250)
                        P.op("dve", lambda e, pd=pd, c=c, t=t: e.scalar_tensor_tensor(
                            out=x[:, c, t * 512:(t + 1) * 512], in0=pd[:], scalar=modv(li, 5, c, g),
                            in1=x[:, c, t * 512:(t + 1) * 512], op0=ALU.mult, op1=ALU.add),
                            reads=[pdk, ("mod", li)] + kx(c, t), writes=kx(c, t))

        rope_state = {"bufs": [None, None], "n": 0}

        def rope_tile(kind, t):
            key = (kind, t)
            for b in range(2):
                if rope_state["bufs"][b] == key:
                    v = TB.view(b * 4096, [2, 512], F32)
                    return v, TB.keys(b * 4096, 4096)
            b = rope_state["n"] % 2
            rope_state["n"] += 1
            rope_state["bufs"][b] = key
            v = TB.view(b * 4096, [2, 512], F32)
            ks = TB.keys(b * 4096, 4096)
            for i in range(2):
                P.dma("sp", v[:, i, :], dr[kind][i][:, t * 512:(t + 1) * 512], writes=ks)
            return v, ks

        def make_rope(kind, perm):
            def cosf(t):
                return rope_tile(kind, t)[0][:, 0, :]

            def sinf(t):
                return rope_tile(kind, t)[0][:, 1, :]

            def ropek(t):
                return rope_tile(kind, t)[1] + ["cm"]
            return (perm, cosf, sinf, ropek)

        NATAB_STRIDE = 6144

        def mixer_mha(kind, li, g, T):
            wo = dr[kind + "_w_o"]
            is_na = kind == "na"
            ngroups = 8 if is_na else 4
            nqc = 1 if is_na else 2
            NV = 128 if is_na else 64
            koff = 1024
            voff = 2048 if is_na else 1280
            sample = g == 1
            ntile = T // 512
            set_ones()
            P.op("dve", lambda e: e.memset(kg[64:128, 0, :], 0.0), writes=U.keys(O_KG, 5120))
            P.op("dve", lambda e: e.memset(kg[0:64, 1, :], 0.0), writes=U.keys(O_KG + 5120, 5120))
            rope = None
            if sample and not is_na:
                rope = make_rope("rope64", cm[:, CM_P64, :])
            rope_state["bufs"] = [None, None]
            qgain = gs[:, GS[kind]:GS[kind] + 1]
            kcol = {"na": SP_NAK, "swa": SP_SWK, "gqa": SP_GQK}[kind]
            kgain = spm[:, kcol:kcol + 1]
            okT = dr["o_" + kind + "_kT"]
            ov = dr["o_" + kind + "_v"]
            kcT = dr[kind + "_kcT"]
            vc = dr[kind + "_vc"]
            ntab = 0
            for grp in range(ngroups):
                b = grp % 2
                base = O_GW[b]
                wq = U.view(base + GW_WQ, [KC, 256], BF16)
                wk = U.view(base + GW_WK, [KC, 128], BF16)
                wv = U.view(base + GW_WV, [KC, NV], BF16)
                wot = U.view(base + GW_WO, [2, 1024], BF16)
                kwq, kwk, kwv, kwo = (U.keys(base + GW_WQ, 4096), U.keys(base + GW_WK, 2048),
                                      U.keys(base + GW_WV, 2048), U.keys(base + GW_WO, 4096))
                QW = 128 if is_na else 256
                if is_na:
                    wq = U.view(base + GW_WQ, [KC, 128], BF16)
                P.dma("pool", U.view(base + GW_WQ, [KC * QW], BF16), dr[kind + "_wq"][grp], writes=kwq, nbytes=QW * 4096)
                P.dma("pool", U.view(base + GW_WK, [KC * 128], BF16), dr[kind + "_wk"][grp], writes=kwk, nbytes=512 << 10)
                P.dma("pool", U.view(base + GW_WV, [KC * NV], BF16), dr[kind + "_wv"][grp], writes=kwv, nbytes=NV * 4096)
                if is_na:
                    P.dma("pool", wot[:, 0, :], wo[grp * 128:(grp + 1) * 128, :], writes=kwo)
                else:
                    P.dma("pool", wot, wo[grp * 256:(grp + 1) * 256, :].rearrange("(k p) n -> p k n", p=128), writes=kwo)
                if sample:
                    if is_na:
                        P.dma("pool", kg[0:64, 0, TS:TS + LCTX], kcT[grp * 128:grp * 128 + 64, :], writes=kk(0, 4))
                        P.dma("pool", kg[64:128, 1, TS:TS + LCTX], kcT[grp * 128 + 64:(grp + 1) * 128, :], writes=kk(1, 4))
                        for hh in range(2):
                            h = 2 * grp + hh
                            P.dma("pool", vaug[:, 16:20, hh, 0:64],
                                  vc[:, h * 64:(h + 1) * 64].rearrange("(a p) f -> p a f", p=128), writes=kva(4))
                    else:
                        for hh in range(2):
                            P.dma("pool", kg[hh * 64:(hh + 1) * 64, hh, TS:TS + LCTX], kcT[grp * 64:(grp + 1) * 64, :],
                                  writes=kk(hh, 4))
                        P.dma("pool", vaug[:, 16:20, 0, 0:64],
                              vc[:, grp * 64:(grp + 1) * 64].rearrange("(a p) f -> p a f", p=128), writes=kva(4))
                for t in range(ntile):
                    cs = slice(t * 512, (t + 1) * 512)
                    for c in range(nqc):
                        proj_fm(128, KC, lambda k, c=c: wq[:, k, c * 128:(c + 1) * 128], lambda k, t: hT[:, k, t * 512:(t + 1) * 512],
                                khT, kwq, [t], cm[:, CM_B64, :], qgain, [("gs", GS[kind])],
                                lambda t, c=c: [(qg[:, c, t * 512:(t + 1) * 512], kq(c, t), slice(0, 128))], rope=rope)
                    kout = None
                    if not sample:
                        if is_na:
                            kout = lambda t, grp=grp: (okT[grp * 128:(grp + 1) * 128, t * 512:(t + 1) * 512], slice(0, 128))
                        else:
                            kout = lambda t, grp=grp: (okT[grp * 64:(grp + 1) * 64, t * 512:(t + 1) * 512], slice(0, 64))
                    proj_fm(128, KC, lambda k: wk[:, k, :], lambda k, t: hT[:, k, t * 512:(t + 1) * 512],
                            khT, kwk, [t], cm[:, CM_B64, :], kgain, ["spm"],
                            lambda t: [(kg[0:64, 0, t * 512:(t + 1) * 512], kk(0, t), slice(0, 64)),
                                       (kg[64:128, 1, t * 512:(t + 1) * 512], kk(1, t), slice(64, 128))], rope=rope, kout=kout)
                if is_na:
                    dest_fn = lambda c0, n: (vaug[:, c0:c0 + n, :, 0:64], kva(c0 // 4))
                    vout = (lambda c0, n, grp=grp: ov[c0 * 128:(c0 + n) * 128, grp * 128:(grp + 1) * 128].rearrange(
                        "(a p) n -> p a n", p=128)) if not sample else None
                else:
                    dest_fn = lambda c0, n: (vaug[:, c0:c0 + n, 0, 0:64], kvr(c0, n))
                    vout = (lambda c0, n, grp=grp: ov[c0 * 128:(c0 + n) * 128, grp * 64:(grp + 1) * 64].rearrange(
                        "(a p) n -> p a n", p=128)) if not sample else None
                proj_tm(NV, KC, lambda k, ch: hT[:, k, ch * 128:(ch + 1) * 128], lambda ch: khT(ch // 4),
                        lambda k: wv[:, k, :], kwv, list(range(T // 128)), dest_fn, vout)
                heads = [(hh * 64, 0, hh, 2 * grp + hh, hh) for hh in range(2)] if is_na else \
                        [((j % 2) * 64, j // 2, 0, 4 * grp + j, j % 2) for j in range(4)]
                for (row0, qc, vslot, h, kslot) in heads:
                    sink_ap = esink[:, h:h + 1] if kind == "swa" else None
                    if not sample:
                        for s in range(T // 256):
                            chunks = [((2 * s + i) * 128, 2 * s + i, 0, 256, []) for i in range(2)]
                            attend(128, 0, qc, kslot, vslot, s * 256, 256, chunks,
                                   qg[row0:row0 + 64, qc, s * 256:(s + 1) * 256], kq(qc, s // 2), sink_ap)
                    else:
                        if is_na:
                            tb = ntab % 2
                            ntab += 1
                            tab = TB.view(tb * NATAB_STRIDE, [NA_NBLK * 64], BF16)
                            tabk = TB.keys(tb * NATAB_STRIDE, NA_NBLK * 64 * 2)
                            P.dma("pool", tab, dr["natab"][h], writes=tabk)
                        for J in range(4):
                            chunks = [(TS + i * 128, 16 + i, 0, 512, []) for i in range(4)]
                            if is_na:
                                for (u, lo_r, hi_r, runs) in NA_PLAN[J]:
                                    rr = [(r0 * 64, nr * 64, tab[:, b0 * 64:(b0 + nr) * 64], tabk) for (r0, nr, b0) in runs]
                                    chunks.append((u * 128, u, lo_r * 64, hi_r * 64, rr))
                            elif kind == "swa":
                                for kb in range(max(4 * J - 1, 0), min(4 * J + 4, 15) + 1):
                                    qlo = max(kb - 1, 4 * J)
                                    qhi = min(kb + 1, 4 * J + 3)
                                    lo, hi = (qlo - 4 * J) * 128, (qhi - 4 * J + 1) * 128
                                    rr = [(lo, hi - lo, swat[:, (qlo - kb + 5) * 128:(qhi - kb + 6) * 128], ["swat"])]
                                    chunks.append((kb * 128, kb, lo, hi, rr))
                            else:
                                chunks += [(u * 128, u, 0, 512, []) for u in range(16)]
                            attend(128, 0, qc, kslot, vslot, J * 512, 512, chunks,
                                   qg[row0:row0 + 64, qc, J * 512:(J + 1) * 512], kq(qc, J), sink_ap)
                wo_group(li, g, wot, kwo, nqc, T)

        def mixer_mla(li, g, T):
            sample = g == 1
            ntile = T // 512
            set_ones()
            rope_state["bufs"] = [None, None]
            rope = make_rope("rope96", cm[0:96, CM_P96, 0:96]) if sample else None
            qlat = U.view(O_QLAT, [2, TS], BF16)
            ckvT = U.view(O_CKV, [TS + LCTX], BF16)

            def kql(c, t):
                return U.keys(O_QLAT + (c * TS + t * 512) * 2, 1024)

            def kckv(t):
                return U.keys(O_CKV + t * 1024, 1024)
            wdq = U.view(O_MW_DQ, [KC, 256], BF16)
            wckv = U.view(O_MW_CKV, [KC, 128], BF16)
            wkpe = U.view(O_MW_KPE, [KC, 96], BF16)
            kwdq, kwckv, kwkpe = U.keys(O_MW_DQ, 4096), U.keys(O_MW_CKV, 2048), U.keys(O_MW_KPE, 1536)
            wdkv = dr["mla_w_dkv"]
            P.dma("pool", U.view(O_MW_DQ, [KC * 256], BF16), dr["mla_wdq"], writes=kwdq, nbytes=1 << 20)
            P.dma("pool", U.view(O_MW_CKV, [KC * 128], BF16), dr["mla_wckv"], writes=kwckv, nbytes=512 << 10)
            P.op("pool", lambda e: e.memset(wkpe[:, :, 0:64], 0.0), writes=kwkpe)
            P.dma("pool", wkpe[:, :, 64:96], wview(wdkv, 0, KC, 128, 32), writes=kwkpe)
            if sample:
                P.dma("pool", ckvT[:, TS:TS + LCTX], dr["mla_ckvT"], writes=kckv(4))
                for s in range(2):
                    P.dma("pool", kg[64:96, s, TS:TS + LCTX], dr["mla_kpeT"], writes=kk(s, 4))
            B96 = cm[0:96, CM_B96, 0:96]
            for t in range(ntile):
                cs = slice(t * 512, (t + 1) * 512)
                pas = []
                pb, pbk = psB.next()
                for c in range(2):
                    pa, pak = psA.next()
                    pas.append((pa, pak))

                    def mm(e, pa=pa, c=c, cs=cs):
                        for k in range(KC):
                            ins = e.matmul(pa[:], wdq[:, k, c * 128:(c + 1) * 128], hT[:, k, cs], start=(k == 0), stop=(k == KC - 1))
                        return ins
                    P.op("pe", mm, reads=kwdq + khT(t), writes=[pak], cost=8 * 250)
                    sq, sk = r_sq.next()
                    P.op("act", lambda e, sq=sq, pa=pa: e.activation(out=sq[:], in_=pa[:], func=AF.Square), reads=[pak], writes=[sk])
                    P.op("pe", lambda e, sq=sq, c=c, pb=pb: e.matmul(pb[:], cm[:, CM_O256, :], sq[:], start=(c == 0), stop=(c == 1)),
                         reads=[sk, "cm"], writes=[pbk])
                rs, rk = r_rstd.next()
                P.op("act", lambda e, rs=rs, pb=pb: e.activation(out=rs[:], in_=pb[:], func=AF.Ln, bias=eps_ap, scale=1.0),
                     reads=[pbk, "spm"], writes=[rk])
                P.op("act", lambda e, rs=rs: e.activation(out=rs[:], in_=rs[:], func=AF.Exp, scale=-0.5), reads=[rk], writes=[rk])
                for c in range(2):
                    pa, pak = pas[c]
                    P.op("dve", lambda e, pa=pa, c=c, rs=rs, cs=cs: e.scalar_tensor_tensor(
                        out=qlat[:, c, cs], in0=pa[:], scalar=spm[:, SP_QLORA + c:SP_QLORA + c + 1], in1=rs[:],
                        op0=ALU.mult, op1=ALU.mult), reads=[pak, rk, "spm"], writes=kql(c, t))
                kout = (lambda t: (dr["o_mla_ckvT"][:, t * 512:(t + 1) * 512], slice(0, 128))) if not sample else None
                proj_fm(128, KC, lambda k: wckv[:, k, :], lambda k, t: hT[:, k, t * 512:(t + 1) * 512], khT, kwckv, [t],
                        cm[:, CM_O128, :], spm[:, SP_KVLORA:SP_KVLORA + 1], ["spm"],
                        lambda t: [(ckvT[:, t * 512:(t + 1) * 512], kckv(t), slice(0, 128))], kout=kout)
                kout = (lambda t: (dr["o_mla_kpeT"][:, t * 512:(t + 1) * 512], slice(64, 96))) if not sample else None
                proj_fm(96, KC, lambda k: wkpe[:, k, :], lambda k, t: hT[:, k, t * 512:(t + 1) * 512], khT, kwkpe, [t],
                        B96, spm[:, SP_K96:SP_K96 + 1], ["spm"],
                        lambda t: [(kg[64:96, s, t * 512:(t + 1) * 512], kk(s, t), slice(64, 96)) for s in range(2)],
                        rope=rope, kout=kout)
            ktiles = list(range(ntile)) + ([4] if sample else [])
            kchunks = list(range(T // 128)) + ([16, 17, 18, 19] if sample else [])
            for grp in range(8):
                b = grp % 2
                base = O_GW[0] + b * 4096
                wuq = U.view(base, [2, 192], BF16)
                wukv = U.view(base + 768, [256], BF16)
                wot = U.view(base + 1280, [1, 1024], BF16)
                kw = U.keys(base, 4096)
                P.dma("pool", wuq, dr["mla_w_uq"][:, grp * 192:(grp + 1) * 192].rearrange("(k p) n -> p k n", p=128), writes=kw)
                P.dma("pool", wukv, dr["mla_w_ukv"][:, grp * 256:(grp + 1) * 256], writes=kw)
                P.dma("pool", wot[:, 0, :], dr["mla_w_o"][grp * 128:(grp + 1) * 128, :], writes=kw)
                for s in range(2):
                    for t in range(ntile):
                        proj_fm(96, 2, lambda k, s=s: wuq[:, k, s * 96:(s + 1) * 96], lambda k, t: qlat[:, k, t * 512:(t + 1) * 512],
                                lambda t: kql(0, t) + kql(1, t), kw, [t], B96, gs[:, GS["mla"]:GS["mla"] + 1], [("gs", GS["mla"])],
                                lambda t, s=s: [(qg[0:96, s, t * 512:(t + 1) * 512], kq(s, t), slice(0, 96))], rope=rope)
                    for t in ktiles:
                        proj_fm(64, 1, lambda k, s=s: wukv[:, s * 128:s * 128 + 64], lambda k, t: ckvT[:, t * 512:(t + 1) * 512],
                                kckv, kw, [t], cm[0:64, CM_B64, 0:64], spm[:, SP_K96:SP_K96 + 1], ["spm"],
                                lambda t, s=s: [(kg[0:64, s, t * 512:(t + 1) * 512], kk(s, t), slice(0, 64))])
                    proj_tm(64, 1, lambda k, ch: ckvT[:, ch * 128:(ch + 1) * 128], lambda ch: kckv(ch // 4),
                            lambda k, s=s: wukv[:, s * 128 + 64:s * 128 + 128], kw, kchunks,
                            lambda c0, n, s=s: (vaug[:, c0:c0 + n, s, 0:64], kvr(c0, n)))
                for s in range(2):
                    if not sample:
                        for sq_ in range(T // 256):
                            chunks = [((2 * sq_ + i) * 128, 2 * sq_ + i, 0, 256, []) for i in range(2)]
                            attend(96, 0, s, s, s, sq_ * 256, 256, chunks,
                                   qg[s * 64:(s + 1) * 64, 0, sq_ * 256:(sq_ + 1) * 256], kq(0, sq_ // 2))
                    else:
                        for J in range(4):
                            chunks = [(TS + i * 128, 16 + i, 0, 512, []) for i in range(4)]
                            chunks += [(u * 128, u, 0, 512, []) for u in range(16)]
                            attend(96, 0, s, s, s, J * 512, 512, chunks,
                                   qg[s * 64:(s + 1) * 64, 0, J * 512:(J + 1) * 512], kq(0, J))
                wo_group(li, g, wot, kw, 1, T)

        for g in passes:
            T = TP if g == 0 else TS
            xin = dr["xpT"] if g == 0 else dr["xsT"]
            yout = dr["ypT"] if g == 0 else dr["ysT"]
            for c in range(KC):
                for t in range(T // 512):
                    P.dma("sp", x[:, c, t * 512:(t + 1) * 512], xin[c * 128:(c + 1) * 128, t * 512:(t + 1) * 512], writes=kx(c, t))
            for li in range(nlayers):
                for t in range(T // 512):
                    norm_tile(li, 1, t, g)
                if g == passes[0] and li + 1 < nlayers:
                    adaln_layer(li + 1)
                if "mixer" in SKIP:
                    pass
                elif li == 0:
                    mixer_mha("na", li, g, T)
                elif li == 1:
                    mixer_mha("swa", li, g, T)
                elif li == 2:
                    mixer_mla(li, g, T)
                else:
                    mixer_mha("gqa", li, g, T)
                if "ffn" not in SKIP:
                    ffn(li, g, T)
            for c in range(KC):
                for t in range(T // 512):
                    P.dma("sp", yout[c * 128:(c + 1) * 128, t * 512:(t + 1) * 512], x[:, c, t * 512:(t + 1) * 512],
                          reads=kx(c, t), is_out=True)
        if SCHED:
            P.schedule()
            nc.sim_time = P.sim_time
        P.emit()
    return nc


_SHARED_CACHE = {}


def _shared_inputs(inp):
    cos64, sin64, perm64, cos96, sin96, perm96 = _rope_tables()
    dr_idx, dc_idx, valid = _na_index_tables()
    rpb = np.asarray(inp["na_rpb"][0], np.float32)
    tab = rpb[:, dr_idx, dc_idx]
    tab = np.where(valid[None], tab, np.float32(NEG)).astype(np.float32)
    natab = np.ascontiguousarray(tab.transpose(0, 2, 1, 3).reshape(16, 128, NA_NBLK * 64))
    sh = {
        "spm": _small_params(inp),
        "cmat": _const_mats(perm64, perm96),
        "natab": natab,
        "swatab": _swa_table(),
        "rope64": np.ascontiguousarray(np.stack([cos64, sin64])),
        "rope96": np.ascontiguousarray(np.stack([cos96, sin96])),
    }
    def pk(w, c0, ncols):
        return np.asarray(w, np.float32)[:, c0:c0 + ncols].reshape(KC, 128, ncols).transpose(1, 0, 2).reshape(128, KC * ncols)
    for li in range(4):
        wm = np.asarray(inp["w_mod"][li], np.float32)
        sh[f"w_modr{li}"] = np.ascontiguousarray(np.stack([pk(wm, b * WMB, WMB) for b in range(6144 // WMB)]))
        wg = np.asarray(inp["ffn_w_gate"][li], np.float32)
        wu = np.asarray(inp["ffn_w_up"][li], np.float32)
        sh[f"ffn_gu{li}"] = np.ascontiguousarray(np.stack(
            [np.concatenate([pk(wg, j * 128, 128), pk(wu, j * 128, 128)], axis=1) for j in range(HC)]))
        wd = np.asarray(inp["ffn_w_down"][li], np.float32)
        sh[f"ffn_d{li}"] = np.ascontiguousarray(np.stack(
            [wd[:, c * 128:(c + 1) * 128].reshape(HC, 128, 128).transpose(1, 0, 2).reshape(128, HC * 128) for c in range(KC)]))
    wq = np.asarray(inp["na_w_qkv"][0], np.float32)
    sh["na_wq"] = np.ascontiguousarray(np.stack([pk(wq, g * 128, 128) for g in range(8)]))
    sh["na_wk"] = np.ascontiguousarray(np.stack([pk(wq, 1024 + g * 128, 128) for g in range(8)]))
    sh["na_wv"] = np.ascontiguousarray(np.stack([pk(wq, 2048 + g * 128, 128) for g in range(8)]))
    for kind in ("swa", "gqa"):
        wq = np.asarray(inp[kind + "_w_qkv"][0], np.float32)
        sh[kind + "_wq"] = np.ascontiguousarray(np.stack([pk(wq, g * 256, 256) for g in range(4)]))
        kd = []
        for g in range(4):
            kk_ = wq[:, 1024 + g * 64:1024 + (g + 1) * 64]
            kd.append(pk(np.concatenate([kk_, kk_], axis=1), 0, 128))
        sh[kind + "_wk"] = np.ascontiguousarray(np.stack(kd))
        sh[kind + "_wv"] = np.ascontiguousarray(np.stack([pk(wq, 1280 + g * 64, 64) for g in range(4)]))
    sh["mla_wdq"] = np.ascontiguousarray(pk(inp["mla_w_dq"][0], 0, 256))
    sh["mla_wckv"] = np.ascontiguousarray(pk(inp["mla_w_dkv"][0], 0, 128))
    for n in ("na_w_o", "swa_w_o", "mla_w_uq", "mla_w_dkv", "mla_w_ukv", "mla_w_o", "gqa_w_o"):
        sh[n] = np.ascontiguousarray(np.asarray(inp[n], np.float32)[0])
    return sh


def _core_inputs(inp, i):
    f = lambda a: np.ascontiguousarray(np.asarray(a, np.float32))
    d = {}
    d["xpT"] = f(inp["x_prompt"][4 * i:4 * i + 4].reshape(TP, D).T)
    d["xsT"] = f(inp["x_sample"][i].T)
    cv = np.zeros((128, KC, 2), np.float32)
    cv[:, :, 0] = np.asarray(inp["c_ctx"]).reshape(KC, 128).T
    cv[:, :, 1] = np.asarray(inp["c"][i]).reshape(KC, 128).T
    d["cv"] = cv.reshape(128, 16)
    d["na_kcT"] = f(inp["cache_na_k"][i, 0].reshape(LCTX, 1024).T)
    d["na_vc"] = f(inp["cache_na_v"][i, 0].reshape(LCTX, 1024))
    d["swa_kcT"] = f(inp["cache_swa_k"][i, 0].reshape(LCTX, 256).T)
    d["swa_vc"] = f(inp["cache_swa_v"][i, 0].reshape(LCTX, 256))
    d["mla_ckvT"] = f(inp["cache_mla_ckv"][i, 0].T)
    d["mla_kpeT"] = f(inp["cache_mla_kpe"][i, 0].T)
    d["gqa_kcT"] = f(inp["cache_gqa_k"][i, 0].reshape(LCTX, 256).T)
    d["gqa_vc"] = f(inp["cache_gqa_v"][i, 0].reshape(LCTX, 256))
    return d


def run(inputs, nlayers=4, passes=(0, 1), trace=False):
    nc = build_program(nlayers=nlayers, passes=passes)
    sh = _shared_inputs(inputs)
    in_maps = []
    used = [n for n in nc.used_inputs if n not in OUT_SHAPES]
    for i in range(NCORES):
        m = dict(sh)
        m.update(_core_inputs(inputs, i))
        in_maps.append({n: m[n] for n in used})
    res = run_bass_kernel_spmd(nc, in_maps, core_ids=list(range(NCORES)), trace=trace)
    R = res.results
    B = 32

    def cat(name):
        return [np.asarray(R[i][name]) for i in range(NCORES)]
    y_p = np.stack([a.T.reshape(4, 256, D) for a in cat("ypT")]).reshape(B, 256, D)
    y_s = np.stack([a.T for a in cat("ysT")])
    na_k = np.stack([a.T.reshape(4, 256, 16, 64) for a in cat("o_na_kT")]).reshape(B, 1, 256, 16, 64)
    na_v = np.stack([a.reshape(4, 256, 16, 64) for a in cat("o_na_v")]).reshape(B, 1, 256, 16, 64)
    swa_k = np.stack([a.T.reshape(4, 256, 4, 64) for a in cat("o_swa_kT")]).reshape(B, 1, 256, 4, 64)
    swa_v = np.stack([a.reshape(4, 256, 4, 64) for a in cat("o_swa_v")]).reshape(B, 1, 256, 4, 64)
    ckv = np.stack([a.T.reshape(4, 256, 128) for a in cat("o_mla_ckvT")]).reshape(B, 1, 256, 128)
    kpe = np.stack([a.T.reshape(4, 256, 32) for a in cat("o_mla_kpeT")]).reshape(B, 1, 256, 32)
    gqa_k = np.stack([a.T.reshape(4, 256, 4, 64) for a in cat("o_gqa_kT")]).reshape(B, 1, 256, 4, 64)
    gqa_v = np.stack([a.reshape(4, 256, 4, 64) for a in cat("o_gqa_v")]).reshape(B, 1, 256, 4, 64)
    outs = (y_p, y_s, na_k, na_v, swa_k, swa_v, ckv, kpe, gqa_k, gqa_v)
    outs = tuple(np.ascontiguousarray(o, dtype=np.float32) for o in outs)
    return outs, res


def kernel(**inputs):
    outs, _ = run(inputs)
    return outs
```

```python
import contextlib
import numpy as np
import concourse.bass as bass
import concourse.mybir as mybir
from concourse.bass_utils import run_bass_kernel_spmd

F32 = mybir.dt.float32
BF16 = mybir.dt.bfloat16
AF = mybir.ActivationFunctionType
ALU = mybir.AluOpType

NCORES = 8
D = 1024
KC = 8
HID = 2816
HC = 22
EPS = 1e-6
NEG = -30000.0
TP = 1024
TS = 2048
LCTX = 512
ENGS = ("pe", "act", "dve", "pool", "sp")
NDMASEM = 12


class Op:
    __slots__ = ("eng", "fn", "deps", "signal", "count", "dma", "dsem", "dval", "idx", "cost", "fin", "seq")

    def __init__(self, eng, fn, dma):
        self.cost = 0.0
        self.fin = None
        self.seq = 0
        self.eng = eng
        self.fn = fn
        self.deps = []
        self.signal = False
        self.count = 0
        self.dma = dma
        self.dsem = None
        self.dval = 0
        self.idx = 0


class Prog:
    def __init__(self, nc):
        self.nc = nc
        self.q = {e: [] for e in ENGS}
        self.last_w = {}
        self.readers = {}
        self.ndma = {e: 0 for e in ENGS}
        self.out_dmas = []
        self.nops = 0

    def op(self, eng, fn, reads=(), writes=(), dma=False, is_out=False, cost=None, n=512):
        o = Op(eng, fn, dma)
        if cost is None:
            if dma:
                cost = 2500.0 + n / 0.15
            else:
                cost = {"pe": 30 + 0.43 * n, "act": 220 + 0.85 * n, "dve": 120 + 1.05 * n, "pool": 300 + 1.0 * n,
                        "sp": 50}[eng]
        o.cost = float(cost)
        o.seq = self.nops
        self.nops += 1
        deps = {}
        for k in reads:
            w = self.last_w.get(k)
            if w is not None:
                deps[id(w)] = w
        for k in writes:
            w = self.last_w.get(k)
            if w is not None:
                deps[id(w)] = w
            for r in self.readers.get(k, ()):
                deps[id(r)] = r
        o.deps = list(deps.values())
        for d in o.deps:
            d.signal = True
        for k in reads:
            self.readers.setdefault(k, []).append(o)
        for k in writes:
            self.last_w[k] = o
            self.readers[k] = []
        if dma:
            self.ndma[eng] += 1
            o.signal = True
            if is_out:
                self.out_dmas.append(o)
        self.q[eng].append(o)
        return o

    def dma(self, eng, out, in_, reads=(), writes=(), is_out=False, nbytes=262144):
        return self.op(eng, lambda e: e.dma_start(out=out, in_=in_), reads, writes, dma=True, is_out=is_out, n=nbytes)

    def schedule(self, window=None):
        window = window or {"pe": 24, "act": 16, "dve": 16, "pool": 8, "sp": 8}
        INF = float("inf")
        qs = {e: list(self.q[e]) for e in ENGS}
        head = {e: 0 for e in ENGS}
        issued = {e: [] for e in ENGS}
        free = {e: 0.0 for e in ENGS}
        taken = set()
        remaining = sum(len(v) for v in qs.values())
        dma_free = 0.0
        while remaining:
            best = None
            for e in ENGS:
                q = qs[e]
                h = head[e]
                while h < len(q) and id(q[h]) in taken:
                    h += 1
                head[e] = h
                cnt = 0
                i = h
                while i < len(q) and cnt < window[e]:
                    o = q[i]
                    i += 1
                    if id(o) in taken:
                        continue
                    cnt += 1
                    rdy = 0.0
                    ok = True
                    for d in o.deps:
                        if d.fin is None:
                            ok = False
                            break
                        if d.fin > rdy:
                            rdy = d.fin
                    if not ok:
                        continue
                    st = rdy if rdy > free[e] else free[e]
                    key = (st, o.seq)
                    if best is None or key < best[0]:
                        best = (key, e, o)
            assert best is not None
            (st, _), e, o = best
            if o.dma:
                issue = 80.0 if e == "sp" else 400.0
                free[e] = st + issue
                xfer = o.cost - 2500.0
                t0 = max(st + 1200.0, dma_free)
                dma_free = t0 + xfer
                o.fin = dma_free + 1300.0
            else:
                free[e] = st + o.cost
                o.fin = free[e] + 60.0
            taken.add(id(o))
            issued[e].append(o)
            remaining -= 1
        self.q = issued
        self.sim_time = max(free.values())

    def emit(self):
        nc = self.nc
        with contextlib.ExitStack() as es:
            esem = {e: es.enter_context(nc.semaphore("es_" + e)) for e in ENGS}
            dsem = {e: [es.enter_context(nc.semaphore(f"ds_{e}_{i}")) for i in range(NDMASEM)]
                    for e in ENGS if self.ndma[e] > 0}
            for e in ENGS:
                c = 0
                di = 0
                for o in self.q[e]:
                    if o.dma:
                        o.idx = di
                        di += 1
                        o.dsem = dsem[e][o.idx % NDMASEM]
                        o.dval = 16 * (o.idx // NDMASEM + 1)
                    elif o.signal:
                        c += 1
                        o.count = c
            block = es.enter_context(nc.Block())
            emap = {"pe": block.tensor, "act": block.scalar, "dve": block.vector, "pool": block.gpsimd,
                    "sp": block.sync}
            for e in ENGS:
                ops = self.q[e]
                final = self.out_dmas if e == "sp" else []

                def body(eng, ops=ops, e=e, final=final):
                    waited = {}

                    def wait(sem, val):
                        k = id(sem)
                        if waited.get(k, 0) >= val:
                            return
                        waited[k] = val
                        eng.wait_ge(sem, val)

                    for o in ops:
                        for d in o.deps:
                            if d.dma:
                                wait(d.dsem, d.dval)
                            else:
                                wait(esem[d.eng], d.count)
                        if o.dma and o.dval > 16:
                            wait(o.dsem, o.dval - 16)
                        ins = o.fn(eng)
                        if o.dma:
                            ins.then_inc(o.dsem, 16)
                        elif o.signal:
                            ins.then_inc(esem[e], 1)
                    for o in final:
                        wait(o.dsem, o.dval)

                emap[e](body)


class Rot:
    def __init__(self, name, tiles, keys=None):
        self.name = name
        self.tiles = tiles
        self.keys = keys if keys is not None else [(name, i) for i in range(len(tiles))]
        self.i = 0

    def next(self):
        i = self.i % len(self.tiles)
        self.i += 1
        return self.tiles[i], self.keys[i]


NA_TINT_LO, NA_TINT_HI = -11, 13
NA_TFULL_LO, NA_TFULL_HI = -7, 8
NA_NINT = NA_TINT_HI - NA_TINT_LO + 1
NA_NFULL = NA_TFULL_HI - NA_TFULL_LO + 1
NA_NBLK = NA_NINT + NA_NFULL


def _na_index_tables():
    kc = np.arange(64)[:, None]
    qc = np.arange(64)[None, :]
    cs = np.clip(qc - 8, 0, 48)
    cvalid = (kc >= cs) & (kc < cs + 16)
    dc = np.clip(kc - qc + 15, 0, 30)
    dr_idx = np.zeros((NA_NBLK, 128, 64), np.int64)
    dc_idx = np.zeros((NA_NBLK, 128, 64), np.int64)
    valid = np.zeros((NA_NBLK, 128, 64), bool)
    for b in range(NA_NBLK):
        if b < NA_NINT:
            delta = NA_TINT_LO + b
            interior = True
        else:
            delta = NA_TFULL_LO + (b - NA_NINT)
            interior = False
        for a in range(2):
            dr = a - delta + 7
            if interior:
                rv = -4 <= (a - delta) <= 3
            else:
                rv = 0 <= dr <= 14
            sl = slice(a * 64, (a + 1) * 64)
            dr_idx[b, sl] = min(max(dr, 0), 14)
            dc_idx[b, sl] = dc
            valid[b, sl] = cvalid & rv
    return dr_idx, dc_idx, valid


def _na_rs(qr):
    return min(max(qr - 4, 0), 24)


def _na_plan():
    plan = []
    for w in range(4):
        rows = list(range(8 * w, 8 * w + 8))
        lst = []
        for u in range(16):
            vr = [qr for qr in rows if any(_na_rs(qr) <= 2 * u + a < _na_rs(qr) + 8 for a in range(2))]
            if not vr:
                continue
            lo, hi = min(vr), max(vr) + 1
            runs = []
            for qr in range(lo, hi):
                delta = qr - 2 * u
                if qr < 4 and u <= 3:
                    blk = NA_NINT + (delta - NA_TFULL_LO)
                elif qr > 28 and u >= 12:
                    blk = NA_NINT + (delta - NA_TFULL_LO)
                else:
                    assert NA_TINT_LO <= delta <= NA_TINT_HI
                    blk = delta - NA_TINT_LO
                if runs and runs[-1][2] + runs[-1][1] == blk:
                    runs[-1][1] += 1
                else:
                    runs.append([qr - 8 * w, 1, blk])
            lst.append((u, lo - 8 * w, hi - 8 * w, [tuple(r) for r in runs]))
        plan.append(lst)
    return plan


NA_PLAN = _na_plan()


def _swa_table():
    ki = np.arange(128)[:, None]
    qi = np.arange(128)[None, :]
    t = np.full((128, 11, 128), NEG, np.float32)
    t[:, 4, :] = np.where(ki <= qi, 0.0, NEG)
    t[:, 5, :] = 0.0
    t[:, 6, :] = np.where(ki >= qi, 0.0, NEG)
    return t.reshape(128, 11 * 128)


def _rope_tables():
    t = np.arange(TS)
    rows = (t // 64).astype(np.float64)
    cols = (t % 64).astype(np.float64)
    cos64 = np.zeros((128, TS)); sin64 = np.zeros((128, TS)); perm64 = np.zeros((128, 128))
    inv16 = 10000.0 ** (-np.arange(16) / 16.0)
    for p in range(128):
        d = p % 64
        pos = rows if d < 32 else cols
        dd = d % 32
        f = dd % 16
        cos64[p] = np.cos(pos.astype(np.float32) * np.float32(inv16[f]))
        sin64[p] = np.sin(pos.astype(np.float32) * np.float32(inv16[f]))
        if dd < 16:
            perm64[p + 16, p] = -1.0
        else:
            perm64[p - 16, p] = 1.0
    cos96 = np.ones((128, TS)); sin96 = np.zeros((128, TS)); perm96 = np.zeros((128, 128))
    inv8 = 10000.0 ** (-np.arange(8) / 8.0)
    for p in range(64, 96):
        d = p - 64
        pos = rows if d < 16 else cols
        dd = d % 16
        f = dd % 8
        cos96[p] = np.cos(pos.astype(np.float32) * np.float32(inv8[f]))
        sin96[p] = np.sin(pos.astype(np.float32) * np.float32(inv8[f]))
        if dd < 8:
            perm96[p + 8, p] = -1.0
        else:
            perm96[p - 8, p] = 1.0
    return (cos64.astype(np.float32), sin64.astype(np.float32), perm64.astype(np.float32),
            cos96.astype(np.float32), sin96.astype(np.float32), perm96.astype(np.float32))


CM_IDENT, CM_B64, CM_B96, CM_O1024, CM_O256, CM_O128, CM_P64, CM_P96 = range(8)
NCM = 8


def _const_mats(perm64, perm96):
    cm = np.zeros((128, NCM, 128), np.float32)
    cm[:, CM_IDENT, :] = np.eye(128)
    b = np.zeros((128, 128)); b[0:64, 0:64] = 1 / 64.0; b[64:128, 64:128] = 1 / 64.0
    cm[:, CM_B64, :] = b
    b = np.zeros((128, 128)); b[0:64, 0:64] = 1 / 64.0; b[64:96, 64:96] = 1 / 32.0
    cm[:, CM_B96, :] = b
    cm[:, CM_O1024, :] = 1 / 1024.0
    cm[:, CM_O256, :] = 1 / 256.0
    cm[:, CM_O128, :] = 1 / 128.0
    cm[:, CM_P64, :] = perm64
    cm[:, CM_P96, :] = perm96
    return cm.reshape(128, NCM * 128)


SP_G1 = 0
SP_G2 = 32
SP_BM = 64
SP_NAQ, SP_NAK, SP_SWQ, SP_SWK, SP_GQQ, SP_GQK = 256, 257, 258, 259, 260, 261
SP_QLORA = 262
SP_KVLORA = 264
SP_Q96 = 265
SP_K96 = 266
SP_SINK = 267
SP_EPS = 283
SP_ZERO = 284
NSP = 285


def _tile64(g):
    return np.concatenate([g, g]).astype(np.float32)


def _small_params(inp):
    sp = np.zeros((128, NSP), np.float32)
    for li in range(4):
        sp[:, SP_G1 + li * 8: SP_G1 + li * 8 + 8] = inp["norm1_g"][li].reshape(8, 128).T
        sp[:, SP_G2 + li * 8: SP_G2 + li * 8 + 8] = inp["norm2_g"][li].reshape(8, 128).T
        sp[:, SP_BM + li * 48: SP_BM + li * 48 + 48] = inp["b_mod"][li].reshape(48, 128).T
    sp[:, SP_NAQ] = _tile64(inp["na_q_g"][0]); sp[:, SP_NAK] = _tile64(inp["na_k_g"][0])
    sp[:, SP_SWQ] = _tile64(inp["swa_q_g"][0]); sp[:, SP_SWK] = _tile64(inp["swa_k_g"][0])
    sp[:, SP_GQQ] = _tile64(inp["gqa_q_g"][0]); sp[:, SP_GQK] = _tile64(inp["gqa_k_g"][0])
    sp[:, SP_QLORA: SP_QLORA + 2] = inp["mla_q_lora_g"][0].reshape(2, 128).T
    sp[:, SP_KVLORA] = inp["mla_kv_lora_g"][0]
    sp[0:64, SP_Q96] = inp["mla_q_nope_g"][0]; sp[64:96, SP_Q96] = inp["mla_q_pe_g"][0]
    sp[0:64, SP_K96] = inp["mla_k_nope_g"][0]; sp[64:96, SP_K96] = inp["mla_k_pe_g"][0]
    sp[:, SP_SINK: SP_SINK + 16] = inp["swa_sink"][0][None, :]
    sp[:, SP_EPS] = EPS
    return sp


W_SHAPES = {
    "na_w_qkv": [1024, 3072], "na_w_o": [1024, 1024],
    "swa_w_qkv": [1024, 1536], "swa_w_o": [1024, 1024],
    "mla_w_dq": [1024, 256], "mla_w_uq": [256, 1536], "mla_w_dkv": [1024, 160],
    "mla_w_ukv": [128, 2048], "mla_w_o": [1024, 1024],
    "gqa_w_qkv": [1024, 1536], "gqa_w_o": [1024, 1024],
}
WMB = 384
for _li in range(4):
    W_SHAPES[f"w_modr{_li}"] = [6144 // WMB, 128, KC * WMB]
    W_SHAPES[f"ffn_gu{_li}"] = [HC, 128, 2 * KC * 128]
    W_SHAPES[f"ffn_d{_li}"] = [KC, 128, HC * 128]
W_SHAPES.update({"na_wq": [8, 128, KC * 128], "na_wk": [8, 128, KC * 128], "na_wv": [8, 128, KC * 128],
                 "swa_wq": [4, 128, KC * 256], "swa_wk": [4, 128, KC * 128], "swa_wv": [4, 128, KC * 64],
                 "gqa_wq": [4, 128, KC * 256], "gqa_wk": [4, 128, KC * 128], "gqa_wv": [4, 128, KC * 64],
                 "mla_wdq": [128, KC * 256], "mla_wckv": [128, KC * 128]})
SKIP = set()
SCHED = True
IN_SHAPES = {
    "xpT": [1024, TP], "xsT": [1024, TS], "cv": [128, 16],
    "na_kcT": [1024, LCTX], "na_vc": [LCTX, 1024],
    "swa_kcT": [256, LCTX], "swa_vc": [LCTX, 256],
    "mla_ckvT": [128, LCTX], "mla_kpeT": [32, LCTX],
    "gqa_kcT": [256, LCTX], "gqa_vc": [LCTX, 256],
    "spm": [128, NSP], "cmat": [128, NCM * 128],
    "natab": [16, 128, NA_NBLK * 64], "swatab": [128, 11 * 128],
    "rope64": [2, 128, TS], "rope96": [2, 128, TS],
}
OUT_SHAPES = {
    "ypT": [1024, TP], "ysT": [1024, TS],
    "o_na_kT": [1024, TP], "o_na_v": [TP, 1024],
    "o_swa_kT": [256, TP], "o_swa_v": [TP, 256],
    "o_mla_ckvT": [128, TP], "o_mla_kpeT": [32, TP],
    "o_gqa_kT": [256, TP], "o_gqa_v": [TP, 256],
}


class Region:
    def __init__(self, name, tensor, nbytes):
        self.name = name
        self.t = tensor
        self.nbytes = nbytes

    def view(self, off, shape, dtype):
        esz = 4 if dtype == F32 else 2
        n = int(np.prod(shape))
        assert off % 4 == 0 and off + n * esz <= self.nbytes, (self.name, off, shape)
        ap = self.t[:, off // 2: off // 2 + n * esz // 2]
        if dtype == F32:
            ap = ap.bitcast(F32)
        if len(shape) == 2:
            ap = ap.rearrange("p (a b) -> p a b", a=shape[0])
        elif len(shape) == 3:
            ap = ap.rearrange("p (a b c) -> p a b c", a=shape[0], b=shape[1])
        return ap

    def keys(self, off, nbytes):
        return [(self.name, s) for s in range(off // 1024, (off + nbytes - 1) // 1024 + 1)]


def build_program(nlayers=4, passes=(0, 1), dbg=False):
    nc = bass.Bass("TRN2", target_bir_lowering=False)
    class LazyDR(dict):
        def __missing__(self, n):
            shp = W_SHAPES[n] if n in W_SHAPES else IN_SHAPES[n]
            v = nc.dram_tensor(n, list(shp), F32, kind="ExternalInput").ap()
            self[n] = v
            return v
    dr = LazyDR()
    for n, s in OUT_SHAPES.items():
        dr[n] = nc.dram_tensor(n, list(s), F32, kind="ExternalOutput").ap()
    nc.used_inputs = dr

    with contextlib.ExitStack() as es:
        def sb(name, shape, dt):
            return es.enter_context(nc.sbuf_tensor("sb_" + name, list(shape), dt))

        P = Prog(nc)
        x = sb("x", [128, KC, TS], F32)
        hT = sb("hT", [128, KC, TS], BF16)
        U_BYTES = 65024
        U = Region("U", sb("U", [128, U_BYTES // 2], BF16), U_BYTES)
        TB_BYTES = 12288
        TB = Region("TB", sb("TB", [128, TB_BYTES // 2], BF16), TB_BYTES)
        swat = sb("swat", [128, 11 * 128], BF16)
        cm = sb("cm", [128, NCM, 128], BF16)
        spm = sb("spm", [128, NSP], F32)
        cvt = sb("cvt", [128, 16], F32)
        silc = sb("silc", [128, KC, 2], BF16)
        mod = sb("mod", [128, 4, 48, 2], F32)
        A1 = sb("A1", [128, 4, KC, 2], F32)
        A2 = sb("A2", [128, 4, KC, 2], F32)
        gs = sb("gs", [128, 8], F32)
        esink = sb("esink", [128, 16], F32)
        psbig = [es.enter_context(nc.psum_tensor(f"ps{i}", [128, 1024], F32)) for i in range(4)]
        psb = [psbig[i // 2][:, (i % 2) * 512:(i % 2 + 1) * 512] for i in range(8)]

        def rot(name, n, shape, dt):
            return Rot(name, [sb(f"{name}{i}", shape, dt) for i in range(n)])

        r_sq = rot("sq", 2, [128, 512], BF16)
        r_rstd = rot("rstd", 3, [128, 512], F32)
        r_qn = rot("qn", 2, [128, 512], BF16)
        r_nrm = rot("nrm", 2, [128, 512], F32)
        r_pT = rot("pT", 2, [128, 1024], BF16)
        r_rden = rot("rden", 2, [128, 512], F32)
        r_sg = rot("sg", 2, [128, 512], BF16)
        r_stage = rot("stage", 2, [128, 512], F32)
        r_t1 = Rot("stage", [r_stage.tiles[0]], [("stage", 0)])
        r_t2 = Rot("stage", [r_stage.tiles[1]], [("stage", 1)])

        class PsRot:
            def __init__(self, banks):
                self.banks = banks
                self.i = 0

            def next(self):
                b = self.banks[self.i % len(self.banks)]
                self.i += 1
                return psb[b], ("ps", b)

        psA = PsRot([0, 1, 4, 6])
        psB = PsRot([2, 3, 5, 7])
        psS = PsRot([4, 5, 6, 7])
        psO = PsRot([0, 1])

        class PsPairRot:
            def __init__(self, pairs):
                self.pairs = pairs
                self.i = 0

            def next(self):
                b = self.pairs[self.i % len(self.pairs)]
                self.i += 1
                return psbig[b], [("ps", 2 * b), ("ps", 2 * b + 1)]
        psS2 = PsPairRot([1, 2, 3])

        eps_ap = spm[:, SP_EPS:SP_EPS + 1]
        zero_ap = spm[:, SP_ZERO:SP_ZERO + 1]

        O_QG, O_KG, O_VA = 0, 8192, 18432
        O_GW = [28672, 40960]
        qg = U.view(O_QG, [2, TS], BF16)
        kg = U.view(O_KG, [2, TS + LCTX], BF16)
        vaug = U.view(O_VA, [20, 2, 128], BF16)
        GW_WQ, GW_WK, GW_WV, GW_WO = 0, 4096, 6144, 8192

        def kq(c, t):
            return U.keys(O_QG + (c * TS + t * 512) * 2, 1024)

        def kk(c, t):
            return U.keys(O_KG + (c * (TS + LCTX) + t * 512) * 2, 1024)

        def kva(ch4):
            return U.keys(O_VA + ch4 * 2048, 2048)

        def kvr(c0, n):
            return U.keys(O_VA + c0 * 512, n * 512)

        O_QLAT, O_CKV = 40960, 49152
        O_MW_DQ, O_MW_CKV, O_MW_KPE = 54272, 58368, 60416
        O_HID, O_WGU, O_WD = 0, 45056, 53248
        hid = U.view(O_HID, [HC, 1024], BF16)

        def khid(j, tl):
            return U.keys(O_HID + (j * 1024 + tl * 512) * 2, 1024)

        def kx(c, t):
            return [("x", c, t)]

        def khT(t):
            return [("hT", t)]

        def wview(w2d, r0, nk, c0, ncols):
            return w2d[r0:r0 + 128 * nk, c0:c0 + ncols].rearrange("(k p) n -> p k n", p=128)

        P.dma("pool", cm[:], dr["cmat"].rearrange("p (a b) -> p a b", a=NCM), writes=["cm"])
        P.dma("sp", spm[:], dr["spm"], writes=["spm"])
        P.dma("sp", cvt[:], dr["cv"], writes=["cv"])
        P.dma("pool", swat[:], dr["swatab"], writes=["swat"])
        vaug3 = U.view(O_VA, [40, 128], BF16)

        def set_ones():
            P.op("dve", lambda e: e.memset(vaug3[:, :, 64:128], 1.0), writes=U.keys(O_VA, 10240))
        for i, (col, sc) in enumerate([(SP_NAQ, 0.125), (SP_SWQ, 0.125), (SP_GQQ, 0.125), (SP_Q96, 96.0 ** -0.5)]):
            P.op("dve", lambda e, i=i, col=col, sc=sc: e.tensor_scalar(out=gs[:, i:i + 1], in0=spm[:, col:col + 1],
                                                                  scalar1=float(sc), scalar2=None, op0=ALU.mult),
                 reads=["spm"], writes=[("gs", i)])
        GS = {"na": 0, "swa": 1, "gqa": 2, "mla": 3}
        P.op("act", lambda e: e.activation(out=esink[:], in_=spm[:, SP_SINK:SP_SINK + 16], func=AF.Exp),
             reads=["spm"], writes=["esink"])

        P.op("act", lambda e: e.activation(out=silc[:].rearrange("p a b -> p (a b)"), in_=cvt[:], func=AF.Silu),
             reads=["cv"], writes=["silc"])
        wmodb = [TB.view(0, [KC, WMB], BF16), TB.view(6144, [KC, WMB], BF16)]
        wmodf = [TB.view(0, [KC * WMB], BF16), TB.view(6144, [KC * WMB], BF16)]
        wmodk = [TB.keys(0, 6144), TB.keys(6144, 6144)]
        adaln_state = {"n": 0}

        def adaln_layer(li):
            for blk in range(6144 // WMB):
                b = adaln_state["n"] % 2
                adaln_state["n"] += 1
                P.dma("pool", wmodf[b], dr[f"w_modr{li}"][blk], writes=wmodk[b], nbytes=WMB * 4096)
                pt, pk = psA.next()
                nj = WMB // 128

                def mm(e, b=b, pt=pt):
                    for jj in range(nj):
                        for k in range(KC):
                            ins = e.matmul(pt[:, jj * 2:jj * 2 + 2], wmodb[b][:, k, jj * 128:(jj + 1) * 128], silc[:, k, :],
                                           start=(k == 0), stop=(k == KC - 1), skip_group_check=True)
                    return ins
                P.op("pe", mm, reads=wmodk[b] + ["silc"], writes=[pk], cost=nj * 8 * 75)
                j0 = blk * nj
                P.op("dve", lambda e, li=li, pt=pt, j0=j0: e.tensor_tensor(
                    out=mod[:, li, j0:j0 + nj, :], in0=pt[:, 0:2 * nj].rearrange("p (a b) -> p a b", b=2),
                    in1=spm[:, SP_BM + li * 48 + j0: SP_BM + li * 48 + j0 + nj].unsqueeze(2).broadcast_to([128, nj, 2]),
                    op=ALU.add), reads=[pk, "spm"], writes=[("mod", li)], n=64)
            for (Aout, v, gcol, nm) in ((A1, 1, SP_G1, "A1"), (A2, 4, SP_G2, "A2")):
                P.op("dve", lambda e, li=li, Aout=Aout, v=v, gcol=gcol: e.scalar_tensor_tensor(
                    out=Aout[:, li], in0=mod[:, li, v * 8:(v + 1) * 8, :], scalar=1.0,
                    in1=spm[:, gcol + li * 8: gcol + li * 8 + 8].unsqueeze(2).broadcast_to([128, 8, 2]),
                    op0=ALU.add, op1=ALU.mult), reads=[("mod", li), "spm"], writes=[(nm, li)], n=64)

        if nlayers > 0:
            adaln_layer(0)

        def modv(li, v, c, g):
            return mod[:, li, v * 8 + c, g:g + 1]

        def norm_tile(li, which, t, g):
            Am = A1 if which == 1 else A2
            vshift = 0 if which == 1 else 3
            cs = slice(t * 512, (t + 1) * 512)
            pt, pk = psB.next()
            xk = []
            for c in range(KC):
                xk += kx(c, t)
            P.op("act", lambda e: e.activation(out=hT[:, :, cs], in_=x[:, :, cs], func=AF.Square),
                 reads=xk, writes=khT(t), n=4096)

            def ssq(e, pt=pt):
                for c in range(KC):
                    ins = e.matmul(pt[:], cm[:, CM_O1024, :], hT[:, c, cs], start=(c == 0), stop=(c == KC - 1))
                return ins
            P.op("pe", ssq, reads=khT(t) + ["cm"], writes=[pk], cost=8 * 250)
            rs, rk = r_rstd.next()
            P.op("act", lambda e, rs=rs, pt=pt: e.activation(out=rs[:], in_=pt[:], func=AF.Ln, bias=eps_ap, scale=1.0),
                 reads=[pk, "spm"], writes=[rk])
            P.op("act", lambda e, rs=rs: e.activation(out=rs[:], in_=rs[:], func=AF.Exp, scale=-0.5), reads=[rk], writes=[rk])
            for c in range(KC):
                nt, nk = r_nrm.next()
                P.op("dve", lambda e, nt=nt, c=c, rs=rs: e.tensor_tensor(out=nt[:], in0=x[:, c, cs], in1=rs[:], op=ALU.mult),
                     reads=kx(c, t) + [rk], writes=[nk])
                P.op("act", lambda e, nt=nt, c=c: e.activation(out=hT[:, c, cs], in_=nt[:], func=AF.Identity,
                                                              bias=modv(li, vshift, c, g), scale=Am[:, li, c, g:g + 1]),
                     reads=[nk, ("mod", li), ("A1" if which == 1 else "A2", li)], writes=khT(t))

        def proj_fm(M, nk, lhs_fn, rhs_fn, rhs_keys_fn, wkeys, tiles, normmat, gain_ap, gain_keys, dests_fn,
                    rope=None, kout=None, r0=0):
            if "pfm" in SKIP:
                return
            for t in tiles:
                pa, pak = psA.next()

                lhs_l = [lhs_fn(k) for k in range(nk)]
                rhs_l = [rhs_fn(k, t) for k in range(nk)]

                def mm(e, pa=pa, lhs_l=lhs_l, rhs_l=rhs_l):
                    for k in range(nk):
                        ins = e.matmul(pa[0:M, :], lhs_l[k], rhs_l[k], start=(k == 0), stop=(k == nk - 1))
                    return ins
                P.op("pe", mm, reads=wkeys + rhs_keys_fn(t), writes=[pak], cost=nk * 250)
                sq, sk = r_sq.next()
                P.op("act", lambda e, sq=sq, pa=pa: e.activation(out=sq[0:M, :], in_=pa[0:M, :], func=AF.Square),
                     reads=[pak], writes=[sk])
                pb, pbk = psB.next()
                P.op("pe", lambda e, pb=pb, sq=sq: e.matmul(pb[0:M, :], normmat, sq[0:M, :], start=True, stop=True),
                     reads=[sk, "cm"], writes=[pbk])
                rs, rk = r_rstd.next()
                P.op("act", lambda e, rs=rs, pb=pb: e.activation(out=rs[0:M, :], in_=pb[0:M, :], func=AF.Ln,
                                                              bias=eps_ap[0:M], scale=1.0),
                     reads=[pbk, "spm"], writes=[rk])
                P.op("act", lambda e, rs=rs: e.activation(out=rs[0:M, :], in_=rs[0:M, :], func=AF.Exp, scale=-0.5),
                     reads=[rk], writes=[rk])
                dests = dests_fn(t)
                if rope is None and kout is None:
                    for (dap, dk, rsl) in dests:
                        P.op("dve", lambda e, dap=dap, rsl=rsl, pa=pa, rs=rs: e.scalar_tensor_tensor(
                            out=dap, in0=pa[rsl, :], scalar=gain_ap[rsl], in1=rs[rsl, :], op0=ALU.mult, op1=ALU.mult),
                            reads=[pak, rk] + gain_keys, writes=dk)
                elif rope is None:
                    st, stk = r_stage.next()
                    P.op("dve", lambda e, st=st, pa=pa, rs=rs: e.scalar_tensor_tensor(
                        out=st[0:M, :], in0=pa[0:M, :], scalar=gain_ap[0:M], in1=rs[0:M, :], op0=ALU.mult, op1=ALU.mult),
                        reads=[pak, rk] + gain_keys, writes=[stk])
                    for (dap, dk, rsl) in dests:
                        P.op("act", lambda e, dap=dap, rsl=rsl, st=st: e.activation(out=dap, in_=st[rsl, :], func=AF.Copy),
                             reads=[stk], writes=dk)
                    kap, krows = kout(t)
                    P.dma("sp", kap, st[krows, :], reads=[stk], is_out=True)
                else:
                    perm, cosf, sinf, ropek = rope
                    rkeys = ropek(t)
                    cos_ap = cosf(t)
                    sin_ap = sinf(t)
                    qn, qk = r_qn.next()
                    P.op("dve", lambda e, qn=qn, pa=pa, rs=rs: e.scalar_tensor_tensor(
                        out=qn[0:M, :], in0=pa[0:M, :], scalar=gain_ap[0:M], in1=rs[0:M, :], op0=ALU.mult, op1=ALU.mult),
                        reads=[pak, rk] + gain_keys, writes=[qk])
                    pc, pck = psB.next()
                    P.op("pe", lambda e, pc=pc, qn=qn: e.matmul(pc[0:M, :], perm, qn[0:M, :], start=True, stop=True),
                         reads=[qk, "cm"], writes=[pck])
                    t1, t1k = r_t1.next()
                    t2, t2k = r_t2.next()
                    P.op("dve", lambda e, t1=t1, qn=qn, cos_ap=cos_ap: e.tensor_tensor(out=t1[0:M, :], in0=qn[0:M, :],
                                                                                  in1=cos_ap[0:M, :], op=ALU.mult),
                         reads=[qk] + rkeys, writes=[t1k])
                    P.op("dve", lambda e, t2=t2, pc=pc, sin_ap=sin_ap: e.tensor_tensor(out=t2[0:M, :], in0=pc[0:M, :],
                                                                                  in1=sin_ap[0:M, :], op=ALU.mult),
                         reads=[pck] + rkeys, writes=[t2k])
                    for (dap, dk, rsl) in dests:
                        P.op("dve", lambda e, dap=dap, rsl=rsl, t1=t1, t2=t2: e.tensor_tensor(out=dap, in0=t1[rsl, :], in1=t2[rsl, :],
                                                                                      op=ALU.add),
                             reads=[t1k, t2k], writes=dk)

        def proj_tm(N, nk, lhs_fn, lhs_keys_fn, rhs_fn, wkeys, chunks, dest_fn, vout=None):
            per = 512 // N
            i = 0
            if "ptm" in SKIP:
                return
            while i < len(chunks):
                grp = chunks[i:i + per]
                assert grp == list(range(grp[0], grp[0] + len(grp)))
                i += per
                pa, pak = psA.next()

                lhs_l = [[lhs_fn(k, ch) for k in range(nk)] for ch in grp]
                rhs_l = [rhs_fn(k) for k in range(nk)]

                def mm(e, pa=pa, lhs_l=lhs_l, rhs_l=rhs_l):
                    for gi in range(len(lhs_l)):
                        for k in range(nk):
                            ins = e.matmul(pa[:, gi * N:(gi + 1) * N], lhs_l[gi][k], rhs_l[k], start=(k == 0),
                                           stop=(k == nk - 1), skip_group_check=True)
                    return ins
                rk = []
                for ch in grp:
                    rk += lhs_keys_fn(ch)
                P.op("pe", mm, reads=wkeys + rk, writes=[pak], cost=len(grp) * nk * (30 + 0.43 * N))
                n = len(grp)
                dap, dk = dest_fn(grp[0], n)
                if len(dap.shape) == 3:
                    P.op("act", lambda e, dap=dap, pa=pa, n=n: e.activation(out=dap, in_=pa[:, 0:n * N].rearrange(
                        "p (a b) -> p a b", a=n), func=AF.Copy), reads=[pak], writes=dk + [pak])
                else:
                    for hh in range(dap.shape[2]):
                        P.op("act", lambda e, dap=dap, pa=pa, n=n, hh=hh: e.activation(
                            out=dap[:, :, hh, :], in_=pa[:, 0:n * N].rearrange("p (a h b) -> p a h b", a=n, b=64)[:, :, hh, :],
                            func=AF.Copy), reads=[pak], writes=dk + [pak])
                if vout is not None and "vout" not in SKIP:
                    st, stk = r_stage.next()
                    P.op("dve", lambda e, st=st, pa=pa, n=n: e.tensor_copy(out=st[:, 0:n * N], in_=pa[:, 0:n * N]),
                         reads=[pak], writes=[stk, pak])
                    vo = vout(grp[0], n)
                    for a_ in range(n):
                        P.dma("sp", vo[:, a_, :], st[:, a_ * N:(a_ + 1) * N], reads=[stk], is_out=True)

        def attend(K, krow0, qc, kc_slot, vslot, q0, nq, chunks, out_ap, out_keys, sink_ap=None):
            if "attn" in SKIP:
                return
            po, pok = psO.next()
            rows = slice(krow0, krow0 + K)
            batches = []
            i = 0
            while i < len(chunks):
                c = chunks[i]
                w = c[3] - c[2]
                if (i + 1 < len(chunks) and not c[4] and not chunks[i + 1][4] and (c[2], c[3]) == (chunks[i + 1][2], chunks[i + 1][3])
                        and (w == 512 or w <= 256)):
                    batches.append([c, chunks[i + 1]])
                    i += 2
                else:
                    batches.append([c])
                    i += 1
            nb_ = len(batches)
            for bi, bat in enumerate(batches):
                lo, hi = bat[0][2], bat[0][3]
                w = hi - lo
                if len(bat) == 2:
                    ps_, psk = psS2.next() if w == 512 else psS.next()
                    if w != 512:
                        psk = [psk]
                    offs = [0, w]
                else:
                    ps_, psk1 = psS.next()
                    psk = [psk1]
                    offs = [lo]
                tot = w * len(bat)
                base = offs[0]

                def mm(e, ps_=ps_, bat=bat, offs=offs, lo=lo, hi=hi, w=w):
                    for j, (kcol0, vch, _, _, runs) in enumerate(bat):
                        ins = e.matmul(ps_[:, offs[j]:offs[j] + w], kg[rows, kc_slot, kcol0:kcol0 + 128],
                                       qg[rows, qc, q0 + lo:q0 + hi], start=True, stop=(len(runs) == 0), skip_group_check=True)
                        for ri, (c0, ncol, rap, _) in enumerate(runs):
                            ins = e.matmul(ps_[:, c0:c0 + ncol], cm[:, CM_IDENT, :], rap, start=False,
                                           stop=(ri == len(runs) - 1), skip_group_check=True)
                    return ins
                rk = kq(qc, q0 // 512) + ["cm"]
                cost = 0.0
                for (kcol0, vch, _, _, runs) in bat:
                    rk += kk(kc_slot, kcol0 // 512)
                    cost += (1 + len(runs)) * (30 + 0.43 * w)
                    for r_ in runs:
                        rk += r_[3]
                P.op("pe", mm, reads=rk, writes=psk, cost=cost)
                pt, ptk = r_pT.next()
                P.op("act", lambda e, pt=pt, ps_=ps_, base=base, tot=tot: e.activation(
                    out=pt[:, base:base + tot], in_=ps_[:, base:base + tot], func=AF.Exp),
                    reads=psk, writes=[ptk], n=tot)

                def pv(e, po=po, pt=pt, bat=bat, offs=offs, lo=lo, hi=hi, w=w, bi=bi):
                    for j, (kcol0, vch, _, _, runs) in enumerate(bat):
                        ins = e.matmul(po[:, lo:hi], vaug[:, vch, vslot, :], pt[:, offs[j]:offs[j] + w],
                                       start=(bi == 0 and j == 0), stop=(bi == nb_ - 1 and j == len(bat) - 1),
                                       skip_group_check=True)
                    return ins
                vk = []
                for (kcol0, vch, _, _, runs) in bat:
                    vk += kva(vch // 4)
                P.op("pe", pv, reads=[ptk] + vk, writes=[pok], cost=len(bat) * (30 + 0.43 * w))
            rd, rdk = r_rden.next()
            bias_ = sink_ap[64:128] if sink_ap is not None else zero_ap[64:128]
            P.op("act", lambda e, rd=rd, po=po, bias_=bias_: e.activation(out=rd[64:128, 0:nq], in_=po[64:128, 0:nq], func=AF.Ln,
                                                                   bias=bias_, scale=1.0),
                 reads=[pok, "esink", "spm"], writes=[rdk], n=nq)
            P.op("act", lambda e, rd=rd: e.activation(out=rd[64:128, 0:nq], in_=rd[64:128, 0:nq], func=AF.Exp, scale=-1.0),
                 reads=[rdk], writes=[rdk], n=nq)
            P.op("dve", lambda e, rd=rd, po=po: e.tensor_tensor(out=out_ap, in0=po[0:64, 0:nq], in1=rd[64:128, 0:nq], op=ALU.mult),
                 reads=[pok, rdk], writes=out_keys)

        def wo_group(li, g, wo_ap, wokeys, nkc, T):
            if "wo" in SKIP:
                return
            for t in range(T // 512):
                for c in range(KC):
                    pa, pak = psA.next()

                    def mm(e, pa=pa, c=c, t=t):
                        for k in range(nkc):
                            ins = e.matmul(pa[:], wo_ap[:, k, c * 128:(c + 1) * 128], qg[:, k, t * 512:(t + 1) * 512],
                                           start=(k == 0), stop=(k == nkc - 1))
                        return ins
                    rk = []
                    for k in range(nkc):
                        rk += kq(k, t)
                    P.op("pe", mm, reads=wokeys + rk, writes=[pak], cost=nkc * 250)
                    P.op("dve", lambda e, pa=pa, c=c, t=t: e.scalar_tensor_tensor(
                        out=x[:, c, t * 512:(t + 1) * 512], in0=pa[:], scalar=modv(li, 2, c, g),
                        in1=x[:, c, t * 512:(t + 1) * 512], op0=ALU.mult, op1=ALU.add),
                        reads=[pak, ("mod", li)] + kx(c, t), writes=kx(c, t))

        def ffn(li, g, T):
            wgu, wd = dr[f"ffn_gu{li}"], dr[f"ffn_d{li}"]
            nb = [0, 0]
            for s in range(T // 1024):
                for tl in range(2):
                    norm_tile(li, 2, s * 2 + tl, g)
                for j in range(HC):
                    b = nb[0] % 2
                    nb[0] += 1
                    wb = U.view(O_WGU + b * 4096, [2, KC, 128], BF16)
                    wbk = U.keys(O_WGU + b * 4096, 4096)
                    P.dma("pool", U.view(O_WGU + b * 4096, [2 * KC * 128], BF16), wgu[j], writes=wbk, nbytes=1 << 20)
                    for tl in range(2):
                        t = s * 2 + tl
                        cs = slice(t * 512, (t + 1) * 512)
                        pg, pgk = psA.next()
                        pu, puk = psB.next()

                        def mm(e, pg=pg, pu=pu, wb=wb, cs=cs):
                            for k in range(KC):
                                e.matmul(pg[:], wb[:, 0, k, :], hT[:, k, cs], start=(k == 0), stop=(k == KC - 1))
                            for k in range(KC):
                                ins = e.matmul(pu[:], wb[:, 1, k, :], hT[:, k, cs], start=(k == 0), stop=(k == KC - 1))
                            return ins
                        P.op("pe", mm, reads=wbk + khT(t), writes=[pgk, puk], cost=16 * 250)
                        sg, sgk = r_sg.next()
                        P.op("act", lambda e, sg=sg, pg=pg: e.activation(out=sg[:], in_=pg[:], func=AF.Silu),
                             reads=[pgk], writes=[sgk])
                        P.op("dve", lambda e, sg=sg, pu=pu, j=j, tl=tl: e.tensor_tensor(
                            out=hid[:, j, tl * 512:(tl + 1) * 512], in0=pu[:], in1=sg[:], op=ALU.mult),
                            reads=[puk, sgk], writes=khid(j, tl))
                for c in range(KC):
                    b = nb[1] % 2
                    nb[1] += 1
                    wdb = U.view(O_WD + b * 6144, [HC, 128], BF16)
                    wdk = U.keys(O_WD + b * 6144, 5632)
                    P.dma("pool", U.view(O_WD + b * 6144, [HC * 128], BF16), wd[c], writes=wdk, nbytes=1408 << 10)
                    for tl in range(2):
                        t = s * 2 + tl
                        pd, pdk = psS.next()

                        def mm(e, pd=pd, wdb=wdb, tl=tl):
                            for j in range(HC):
                                ins = e.matmul(pd[:], wdb[:, j, :], hid[:, j, tl * 512:(tl + 1) * 512], start=(j == 0),
                                               stop=(j == HC - 1))
                            return ins
                        rk = []
                        for j in range(HC):
                            rk += khid(j, tl)
                        P.op("pe", mm, reads=wdk + rk, writes=[pdk], cost=22 * 250)
                        P.op("dve", lambda e, pd=pd, c=c, t=t: e.scalar_tensor_tensor(
                            out=x[:, c, t * 512:(t + 1) * 512], in0=pd[:], scalar=modv(li, 5, c, g),
                            in1=x[:, c, t * 512:(t + 1) * 512], op0=ALU.mult, op1=ALU.add),
                            reads=[pdk, ("mod", li)] + kx(c, t), writes=kx(c, t))

        rope_state = {"bufs": [None, None], "n": 0}

        def rope_tile(kind, t):
            key = (kind, t)
            for b in range(2):
                if rope_state["bufs"][b] == key:
                    v = TB.view(b * 4096, [2, 512], F32)
                    return v, TB.keys(b * 4096, 4096)
            b = rope_state["n"] % 2
            rope_state["n"] += 1
            rope_state["bufs"][b] = key
            v = TB.view(b * 4096, [2, 512], F32)
            ks = TB.keys(b * 4096, 4096)
            for i in range(2):
                P.dma("sp", v[:, i, :], dr[kind][i][:, t * 512:(t + 1) * 512], writes=ks)
            return v, ks

        def make_rope(kind, perm):
            def cosf(t):
                return rope_tile(kind, t)[0][:, 0, :]

            def sinf(t):
                return rope_tile(kind, t)[0][:, 1, :]

            def ropek(t):
                return rope_tile(kind, t)[1] + ["cm"]
            return (perm, cosf, sinf, ropek)

        NATAB_STRIDE = 6144

        def mixer_mha(kind, li, g, T):
            wo = dr[kind + "_w_o"]
            is_na = kind == "na"
            ngroups = 8 if is_na else 4
            nqc = 1 if is_na else 2
            NV = 128 if is_na else 64
            koff = 1024
            voff = 2048 if is_na else 1280
            sample = g == 1
            ntile = T // 512
            set_ones()
            P.op("dve", lambda e: e.memset(kg[64:128, 0, :], 0.0), writes=U.keys(O_KG, 5120))
            P.op("dve", lambda e: e.memset(kg[0:64, 1, :], 0.0), writes=U.keys(O_KG + 5120, 5120))
            rope = None
            if sample and not is_na:
                rope = make_rope("rope64", cm[:, CM_P64, :])
            rope_state["bufs"] = [None, None]
            qgain = gs[:, GS[kind]:GS[kind] + 1]
            kcol = {"na": SP_NAK, "swa": SP_SWK, "gqa": SP_GQK}[kind]
            kgain = spm[:, kcol:kcol + 1]
            okT = dr["o_" + kind + "_kT"]
            ov = dr["o_" + kind + "_v"]
            kcT = dr[kind + "_kcT"]
            vc = dr[kind + "_vc"]
            ntab = 0
            for grp in range(ngroups):
                b = grp % 2
                base = O_GW[b]
                wq = U.view(base + GW_WQ, [KC, 256], BF16)
                wk = U.view(base + GW_WK, [KC, 128], BF16)
                wv = U.view(base + GW_WV, [KC, NV], BF16)
                wot = U.view(base + GW_WO, [2, 1024], BF16)
                kwq, kwk, kwv, kwo = (U.keys(base + GW_WQ, 4096), U.keys(base + GW_WK, 2048),
                                      U.keys(base + GW_WV, 2048), U.keys(base + GW_WO, 4096))
                QW = 128 if is_na else 256
                if is_na:
                    wq = U.view(base + GW_WQ, [KC, 128], BF16)
                P.dma("pool", U.view(base + GW_WQ, [KC * QW], BF16), dr[kind + "_wq"][grp], writes=kwq, nbytes=QW * 4096)
                P.dma("pool", U.view(base + GW_WK, [KC * 128], BF16), dr[kind + "_wk"][grp], writes=kwk, nbytes=512 << 10)
                P.dma("pool", U.view(base + GW_WV, [KC * NV], BF16), dr[kind + "_wv"][grp], writes=kwv, nbytes=NV * 4096)
                if is_na:
                    P.dma("pool", wot[:, 0, :], wo[grp * 128:(grp + 1) * 128, :], writes=kwo)
                else:
                    P.dma("pool", wot, wo[grp * 256:(grp + 1) * 256, :].rearrange("(k p) n -> p k n", p=128), writes=kwo)
                if sample:
                    if is_na:
                        P.dma("pool", kg[0:64, 0, TS:TS + LCTX], kcT[grp * 128:grp * 128 + 64, :], writes=kk(0, 4))
                        P.dma("pool", kg[64:128, 1, TS:TS + LCTX], kcT[grp * 128 + 64:(grp + 1) * 128, :], writes=kk(1, 4))
                        for hh in range(2):
                            h = 2 * grp + hh
                            P.dma("pool", vaug[:, 16:20, hh, 0:64],
                                  vc[:, h * 64:(h + 1) * 64].rearrange("(a p) f -> p a f", p=128), writes=kva(4))
                    else:
                        for hh in range(2):
                            P.dma("pool", kg[hh * 64:(hh + 1) * 64, hh, TS:TS + LCTX], kcT[grp * 64:(grp + 1) * 64, :],
                                  writes=kk(hh, 4))
                        P.dma("pool", vaug[:, 16:20, 0, 0:64],
                              vc[:, grp * 64:(grp + 1) * 64].rearrange("(a p) f -> p a f", p=128), writes=kva(4))
                for t in range(ntile):
                    cs = slice(t * 512, (t + 1) * 512)
                    for c in range(nqc):
                        proj_fm(128, KC, lambda k, c=c: wq[:, k, c * 128:(c + 1) * 128], lambda k, t: hT[:, k, t * 512:(t + 1) * 512],
                                khT, kwq, [t], cm[:, CM_B64, :], qgain, [("gs", GS[kind])],
                                lambda t, c=c: [(qg[:, c, t * 512:(t + 1) * 512], kq(c, t), slice(0, 128))], rope=rope)
                    kout = None
                    if not sample:
                        if is_na:
                            kout = lambda t, grp=grp: (okT[grp * 128:(grp + 1) * 128, t * 512:(t + 1) * 512], slice(0, 128))
                        else:
                            kout = lambda t, grp=grp: (okT[grp * 64:(grp + 1) * 64, t * 512:(t + 1) * 512], slice(0, 64))
                    proj_fm(128, KC, lambda k: wk[:, k, :], lambda k, t: hT[:, k, t * 512:(t + 1) * 512],
                            khT, kwk, [t], cm[:, CM_B64, :], kgain, ["spm"],
                            lambda t: [(kg[0:64, 0, t * 512:(t + 1) * 512], kk(0, t), slice(0, 64)),
                                       (kg[64:128, 1, t * 512:(t + 1) * 512], kk(1, t), slice(64, 128))], rope=rope, kout=kout)
                if is_na:
                    dest_fn = lambda c0, n: (vaug[:, c0:c0 + n, :, 0:64], kva(c0 // 4))
                    vout = (lambda c0, n, grp=grp: ov[c0 * 128:(c0 + n) * 128, grp * 128:(grp + 1) * 128].rearrange(
                        "(a p) n -> p a n", p=128)) if not sample else None
                else:
                    dest_fn = lambda c0, n: (vaug[:, c0:c0 + n, 0, 0:64], kvr(c0, n))
                    vout = (lambda c0, n, grp=grp: ov[c0 * 128:(c0 + n) * 128, grp * 64:(grp + 1) * 64].rearrange(
                        "(a p) n -> p a n", p=128)) if not sample else None
                proj_tm(NV, KC, lambda k, ch: hT[:, k, ch * 128:(ch + 1) * 128], lambda ch: khT(ch // 4),
                        lambda k: wv[:, k, :], kwv, list(range(T // 128)), dest_fn, vout)
                heads = [(hh * 64, 0, hh, 2 * grp + hh, hh) for hh in range(2)] if is_na else \
                        [((j % 2) * 64, j // 2, 0, 4 * grp + j, j % 2) for j in range(4)]
                for (row0, qc, vslot, h, kslot) in heads:
                    sink_ap = esink[:, h:h + 1] if kind == "swa" else None
                    if not sample:
                        for s in range(T // 256):
                            chunks = [((2 * s + i) * 128, 2 * s + i, 0, 256, []) for i in range(2)]
                            attend(128, 0, qc, kslot, vslot, s * 256, 256, chunks,
                                   qg[row0:row0 + 64, qc, s * 256:(s + 1) * 256], kq(qc, s // 2), sink_ap)
                    else:
                        if is_na:
                            tb = ntab % 2
                            ntab += 1
                            tab = TB.view(tb * NATAB_STRIDE, [NA_NBLK * 64], BF16)
                            tabk = TB.keys(tb * NATAB_STRIDE, NA_NBLK * 64 * 2)
                            P.dma("pool", tab, dr["natab"][h], writes=tabk)
                        for J in range(4):
                            chunks = [(TS + i * 128, 16 + i, 0, 512, []) for i in range(4)]
                            if is_na:
                                for (u, lo_r, hi_r, runs) in NA_PLAN[J]:
                                    rr = [(r0 * 64, nr * 64, tab[:, b0 * 64:(b0 + nr) * 64], tabk) for (r0, nr, b0) in runs]
                                    chunks.append((u * 128, u, lo_r * 64, hi_r * 64, rr))
                            elif kind == "swa":
                                for kb in range(max(4 * J - 1, 0), min(4 * J + 4, 15) + 1):
                                    qlo = max(kb - 1, 4 * J)
                                    qhi = min(kb + 1, 4 * J + 3)
                                    lo, hi = (qlo - 4 * J) * 128, (qhi - 4 * J + 1) * 128
                                    rr = [(lo, hi - lo, swat[:, (qlo - kb + 5) * 128:(qhi - kb + 6) * 128], ["swat"])]
                                    chunks.append((kb * 128, kb, lo, hi, rr))
                            else:
                                chunks += [(u * 128, u, 0, 512, []) for u in range(16)]
                            attend(128, 0, qc, kslot, vslot, J * 512, 512, chunks,
                                   qg[row0:row0 + 64, qc, J * 512:(J + 1) * 512], kq(qc, J), sink_ap)
                wo_group(li, g, wot, kwo, nqc, T)

        def mixer_mla(li, g, T):
            sample = g == 1
            ntile = T // 512
            set_ones()
            rope_state["bufs"] = [None, None]
            rope = make_rope("rope96", cm[0:96, CM_P96, 0:96]) if sample else None
            qlat = U.view(O_QLAT, [2, TS], BF16)
            ckvT = U.view(O_CKV, [TS + LCTX], BF16)

            def kql(c, t):
                return U.keys(O_QLAT + (c * TS + t * 512) * 2, 1024)

            def kckv(t):
                return U.keys(O_CKV + t * 1024, 1024)
            wdq = U.view(O_MW_DQ, [KC, 256], BF16)
            wckv = U.view(O_MW_CKV, [KC, 128], BF16)
            wkpe = U.view(O_MW_KPE, [KC, 96], BF16)
            kwdq, kwckv, kwkpe = U.keys(O_MW_DQ, 4096), U.keys(O_MW_CKV, 2048), U.keys(O_MW_KPE, 1536)
            wdkv = dr["mla_w_dkv"]
            P.dma("pool", U.view(O_MW_DQ, [KC * 256], BF16), dr["mla_wdq"], writes=kwdq, nbytes=1 << 20)
            P.dma("pool", U.view(O_MW_CKV, [KC * 128], BF16), dr["mla_wckv"], writes=kwckv, nbytes=512 << 10)
            P.op("pool", lambda e: e.memset(wkpe[:, :, 0:64], 0.0), writes=kwkpe)
            P.dma("pool", wkpe[:, :, 64:96], wview(wdkv, 0, KC, 128, 32), writes=kwkpe)
            if sample:
                P.dma("pool", ckvT[:, TS:TS + LCTX], dr["mla_ckvT"], writes=kckv(4))
                for s in range(2):
                    P.dma("pool", kg[64:96, s, TS:TS + LCTX], dr["mla_kpeT"], writes=kk(s, 4))
            B96 = cm[0:96, CM_B96, 0:96]
            for t in range(ntile):
                cs = slice(t * 512, (t + 1) * 512)
                pas = []
                pb, pbk = psB.next()
                for c in range(2):
                    pa, pak = psA.next()
                    pas.append((pa, pak))

                    def mm(e, pa=pa, c=c, cs=cs):
                        for k in range(KC):
                            ins = e.matmul(pa[:], wdq[:, k, c * 128:(c + 1) * 128], hT[:, k, cs], start=(k == 0), stop=(k == KC - 1))
                        return ins
                    P.op("pe", mm, reads=kwdq + khT(t), writes=[pak], cost=8 * 250)
                    sq, sk = r_sq.next()
                    P.op("act", lambda e, sq=sq, pa=pa: e.activation(out=sq[:], in_=pa[:], func=AF.Square), reads=[pak], writes=[sk])
                    P.op("pe", lambda e, sq=sq, c=c, pb=pb: e.matmul(pb[:], cm[:, CM_O256, :], sq[:], start=(c == 0), stop=(c == 1)),
                         reads=[sk, "cm"], writes=[pbk])
                rs, rk = r_rstd.next()
                P.op("act", lambda e, rs=rs, pb=pb: e.activation(out=rs[:], in_=pb[:], func=AF.Ln, bias=eps_ap, scale=1.0),
                     reads=[pbk, "spm"], writes=[rk])
                P.op("act", lambda e, rs=rs: e.activation(out=rs[:], in_=rs[:], func=AF.Exp, scale=-0.5), reads=[rk], writes=[rk])
                for c in range(2):
                    pa, pak = pas[c]
                    P.op("dve", lambda e, pa=pa, c=c, rs=rs, cs=cs: e.scalar_tensor_tensor(
                        out=qlat[:, c, cs], in0=pa[:], scalar=spm[:, SP_QLORA + c:SP_QLORA + c + 1], in1=rs[:],
                        op0=ALU.mult, op1=ALU.mult), reads=[pak, rk, "spm"], writes=kql(c, t))
                kout = (lambda t: (dr["o_mla_ckvT"][:, t * 512:(t + 1) * 512], slice(0, 128))) if not sample else None
                proj_fm(128, KC, lambda k: wckv[:, k, :], lambda k, t: hT[:, k, t * 512:(t + 1) * 512], khT, kwckv, [t],
                        cm[:, CM_O128, :], spm[:, SP_KVLORA:SP_KVLORA + 1], ["spm"],
                        lambda t: [(ckvT[:, t * 512:(t + 1) * 512], kckv(t), slice(0, 128))], kout=kout)
                kout = (lambda t: (dr["o_mla_kpeT"][:, t * 512:(t + 1) * 512], slice(64, 96))) if not sample else None
                proj_fm(96, KC, lambda k: wkpe[:, k, :], lambda k, t: hT[:, k, t * 512:(t + 1) * 512], khT, kwkpe, [t],
                        B96, spm[:, SP_K96:SP_K96 + 1], ["spm"],
                        lambda t: [(kg[64:96, s, t * 512:(t + 1) * 512], kk(s, t), slice(64, 96)) for s in range(2)],
                        rope=rope, kout=kout)
            ktiles = list(range(ntile)) + ([4] if sample else [])
            kchunks = list(range(T // 128)) + ([16, 17, 18, 19] if sample else [])
            for grp in range(8):
                b = grp % 2
                base = O_GW[0] + b * 4096
                wuq = U.view(base, [2, 192], BF16)
                wukv = U.view(base + 768, [256], BF16)
                wot = U.view(base + 1280, [1, 1024], BF16)
                kw = U.keys(base, 4096)
                P.dma("pool", wuq, dr["mla_w_uq"][:, grp * 192:(grp + 1) * 192].rearrange("(k p) n -> p k n", p=128), writes=kw)
                P.dma("pool", wukv, dr["mla_w_ukv"][:, grp * 256:(grp + 1) * 256], writes=kw)
                P.dma("pool", wot[:, 0, :], dr["mla_w_o"][grp * 128:(grp + 1) * 128, :], writes=kw)
                for s in range(2):
                    for t in range(ntile):
                        proj_fm(96, 2, lambda k, s=s: wuq[:, k, s * 96:(s + 1) * 96], lambda k, t: qlat[:, k, t * 512:(t + 1) * 512],
                                lambda t: kql(0, t) + kql(1, t), kw, [t], B96, gs[:, GS["mla"]:GS["mla"] + 1], [("gs", GS["mla"])],
                                lambda t, s=s: [(qg[0:96, s, t * 512:(t + 1) * 512], kq(s, t), slice(0, 96))], rope=rope)
                    for t in ktiles:
                        proj_fm(64, 1, lambda k, s=s: wukv[:, s * 128:s * 128 + 64], lambda k, t: ckvT[:, t * 512:(t + 1) * 512],
                                kckv, kw, [t], cm[0:64, CM_B64, 0:64], spm[:, SP_K96:SP_K96 + 1], ["spm"],
                                lambda t, s=s: [(kg[0:64, s, t * 512:(t + 1) * 512], kk(s, t), slice(0, 64))])
                    proj_tm(64, 1, lambda k, ch: ckvT[:, ch * 128:(ch + 1) * 128], lambda ch: kckv(ch // 4),
                            lambda k, s=s: wukv[:, s * 128 + 64:s * 128 + 128], kw, kchunks,
                            lambda c0, n, s=s: (vaug[:, c0:c0 + n, s, 0:64], kvr(c0, n)))
                for s in range(2):
                    if not sample:
                        for sq_ in range(T // 256):
                            chunks = [((2 * sq_ + i) * 128, 2 * sq_ + i, 0, 256, []) for i in range(2)]
                            attend(96, 0, s, s, s, sq_ * 256, 256, chunks,
                                   qg[s * 64:(s + 1) * 64, 0, sq_ * 256:(sq_ + 1) * 256], kq(0, sq_ // 2))
                    else:
                        for J in range(4):
                            chunks = [(TS + i * 128, 16 + i, 0, 512, []) for i in range(4)]
                            chunks += [(u * 128, u, 0, 512, []) for u in range(16)]
                            attend(96, 0, s, s, s, J * 512, 512, chunks,
                                   qg[s * 64:(s + 1) * 64, 0, J * 512:(J + 1) * 512], kq(0, J))
                wo_group(li, g, wot, kw, 1, T)

        for g in passes:
            T = TP if g == 0 else TS
            xin = dr["xpT"] if g == 0 else dr["xsT"]
            yout = dr["ypT"] if g == 0 else dr["ysT"]
            for c in range(KC):
                for t in range(T // 512):
                    P.dma("sp", x[:, c, t * 512:(t + 1) * 512], xin[c * 128:(c + 1) * 128, t * 512:(t + 1) * 512], writes=kx(c, t))
            for li in range(nlayers):
                for t in range(T // 512):
                    norm_tile(li, 1, t, g)
                if g == passes[0] and li + 1 < nlayers:
                    adaln_layer(li + 1)
                if "mixer" in SKIP:
                    pass
                elif li == 0:
                    mixer_mha("na", li, g, T)
                elif li == 1:
                    mixer_mha("swa", li, g, T)
                elif li == 2:
                    mixer_mla(li, g, T)
                else:
                    mixer_mha("gqa", li, g, T)
                if "ffn" not in SKIP:
                    ffn(li, g, T)
            for c in range(KC):
                for t in range(T // 512):
                    P.dma("sp", yout[c * 128:(c + 1) * 128, t * 512:(t + 1) * 512], x[:, c, t * 512:(t + 1) * 512],
                          reads=kx(c, t), is_out=True)
        if SCHED:
            P.schedule()
            nc.sim_time = P.sim_time
        P.emit()
    return nc


_SHARED_CACHE = {}


def _shared_inputs(inp):
    cos64, sin64, perm64, cos96, sin96, perm96 = _rope_tables()
    dr_idx, dc_idx, valid = _na_index_tables()
    rpb = np.asarray(inp["na_rpb"][0], np.float32)
    tab = rpb[:, dr_idx, dc_idx]
    tab = np.where(valid[None], tab, np.float32(NEG)).astype(np.float32)
    natab = np.ascontiguousarray(tab.transpose(0, 2, 1, 3).reshape(16, 128, NA_NBLK * 64))
    sh = {
        "spm": _small_params(inp),
        "cmat": _const_mats(perm64, perm96),
        "natab": natab,
        "swatab": _swa_table(),
        "rope64": np.ascontiguousarray(np.stack([cos64, sin64])),
        "rope96": np.ascontiguousarray(np.stack([cos96, sin96])),
    }
    def pk(w, c0, ncols):
        return np.asarray(w, np.float32)[:, c0:c0 + ncols].reshape(KC, 128, ncols).transpose(1, 0, 2).reshape(128, KC * ncols)
    for li in range(4):
        wm = np.asarray(inp["w_mod"][li], np.float32)
        sh[f"w_modr{li}"] = np.ascontiguousarray(np.stack([pk(wm, b * WMB, WMB) for b in range(6144 // WMB)]))
        wg = np.asarray(inp["ffn_w_gate"][li], np.float32)
        wu = np.asarray(inp["ffn_w_up"][li], np.float32)
        sh[f"ffn_gu{li}"] = np.ascontiguousarray(np.stack(
            [np.concatenate([pk(wg, j * 128, 128), pk(wu, j * 128, 128)], axis=1) for j in range(HC)]))
        wd = np.asarray(inp["ffn_w_down"][li], np.float32)
        sh[f"ffn_d{li}"] = np.ascontiguousarray(np.stack(
            [wd[:, c * 128:(c + 1) * 128].reshape(HC, 128, 128).transpose(1, 0, 2).reshape(128, HC * 128) for c in range(KC)]))
    wq = np.asarray(inp["na_w_qkv"][0], np.float32)
    sh["na_wq"] = np.ascontiguousarray(np.stack([pk(wq, g * 128, 128) for g in range(8)]))
    sh["na_wk"] = np.ascontiguousarray(np.stack([pk(wq, 1024 + g * 128, 128) for g in range(8)]))
    sh["na_wv"] = np.ascontiguousarray(np.stack([pk(wq, 2048 + g * 128, 128) for g in range(8)]))
    for kind in ("swa", "gqa"):
        wq = np.asarray(inp[kind + "_w_qkv"][0], np.float32)
        sh[kind + "_wq"] = np.ascontiguousarray(np.stack([pk(wq, g * 256, 256) for g in range(4)]))
        kd = []
        for g in range(4):
            kk_ = wq[:, 1024 + g * 64:1024 + (g + 1) * 64]
            kd.append(pk(np.concatenate([kk_, kk_], axis=1), 0, 128))
        sh[kind + "_wk"] = np.ascontiguousarray(np.stack(kd))
        sh[kind + "_wv"] = np.ascontiguousarray(np.stack([pk(wq, 1280 + g * 64, 64) for g in range(4)]))
    sh["mla_wdq"] = np.ascontiguousarray(pk(inp["mla_w_dq"][0], 0, 256))
    sh["mla_wckv"] = np.ascontiguousarray(pk(inp["mla_w_dkv"][0], 0, 128))
    for n in ("na_w_o", "swa_w_o", "mla_w_uq", "mla_w_dkv", "mla_w_ukv", "mla_w_o", "gqa_w_o"):
        sh[n] = np.ascontiguousarray(np.asarray(inp[n], np.float32)[0])
    return sh


def _core_inputs(inp, i):
    f = lambda a: np.ascontiguousarray(np.asarray(a, np.float32))
    d = {}
    d["xpT"] = f(inp["x_prompt"][4 * i:4 * i + 4].reshape(TP, D).T)
    d["xsT"] = f(inp["x_sample"][i].T)
    cv = np.zeros((128, KC, 2), np.float32)
    cv[:, :, 0] = np.asarray(inp["c_ctx"]).reshape(KC, 128).T
    cv[:, :, 1] = np.asarray(inp["c"][i]).reshape(KC, 128).T
    d["cv"] = cv.reshape(128, 16)
    d["na_kcT"] = f(inp["cache_na_k"][i, 0].reshape(LCTX, 1024).T)
    d["na_vc"] = f(inp["cache_na_v"][i, 0].reshape(LCTX, 1024))
    d["swa_kcT"] = f(inp["cache_swa_k"][i, 0].reshape(LCTX, 256).T)
    d["swa_vc"] = f(inp["cache_swa_v"][i, 0].reshape(LCTX, 256))
    d["mla_ckvT"] = f(inp["cache_mla_ckv"][i, 0].T)
    d["mla_kpeT"] = f(inp["cache_mla_kpe"][i, 0].T)
    d["gqa_kcT"] = f(inp["cache_gqa_k"][i, 0].reshape(LCTX, 256).T)
    d["gqa_vc"] = f(inp["cache_gqa_v"][i, 0].reshape(LCTX, 256))
    return d


def run(inputs, nlayers=4, passes=(0, 1), trace=False):
    nc = build_program(nlayers=nlayers, passes=passes)
    sh = _shared_inputs(inputs)
    in_maps = []
    used = [n for n in nc.used_inputs if n not in OUT_SHAPES]
    for i in range(NCORES):
        m = dict(sh)
        m.update(_core_inputs(inputs, i))
        in_maps.append({n: m[n] for n in used})
    res = run_bass_kernel_spmd(nc, in_maps, core_ids=list(range(NCORES)), trace=trace)
    R = res.results
    B = 32

    def cat(name):
        return [np.asarray(R[i][name]) for i in range(NCORES)]
    y_p = np.stack([a.T.reshape(4, 256, D) for a in cat("ypT")]).reshape(B, 256, D)
    y_s = np.stack([a.T for a in cat("ysT")])
    na_k = np.stack([a.T.reshape(4, 256, 16, 64) for a in cat("o_na_kT")]).reshape(B, 1, 256, 16, 64)
    na_v = np.stack([a.reshape(4, 256, 16, 64) for a in cat("o_na_v")]).reshape(B, 1, 256, 16, 64)
    swa_k = np.stack([a.T.reshape(4, 256, 4, 64) for a in cat("o_swa_kT")]).reshape(B, 1, 256, 4, 64)
    swa_v = np.stack([a.reshape(4, 256, 4, 64) for a in cat("o_swa_v")]).reshape(B, 1, 256, 4, 64)
    ckv = np.stack([a.T.reshape(4, 256, 128) for a in cat("o_mla_ckvT")]).reshape(B, 1, 256, 128)
    kpe = np.stack([a.T.reshape(4, 256, 32) for a in cat("o_mla_kpeT")]).reshape(B, 1, 256, 32)
    gqa_k = np.stack([a.T.reshape(4, 256, 4, 64) for a in cat("o_gqa_kT")]).reshape(B, 1, 256, 4, 64)
    gqa_v = np.stack([a.reshape(4, 256, 4, 64) for a in cat("o_gqa_v")]).reshape(B, 1, 256, 4, 64)
    outs = (y_p, y_s, na_k, na_v, swa_k, swa_v, ckv, kpe, gqa_k, gqa_v)
    outs = tuple(np.ascontiguousarray(o, dtype=np.float32) for o in outs)
    return outs, res


def kernel(**inputs):
    outs, _ = run(inputs)
    return outs
```
